# Optimizing a Trainium2 kernel written in Bass

```python
import jax, jax.numpy as jnp
from jax import lax
import numpy as np

D_MODEL = 1024
BATCH = 4
SEQ = 4096
DEPTH = 4

GRID_W = 64
NA_HEADS = 8
NA_HEAD_DIM = 64
NA_WIN_ROWS = 8
NA_WIN_COLS = 16
D_NA = NA_HEADS * NA_HEAD_DIM
SGU_GROUPS = 8
SGU_GROUP_DIM = 64
SGU_CHUNK = 128
D_SGU = SGU_GROUPS * SGU_GROUP_DIM
D_MIX = D_NA + D_SGU
D_IN = 3 * D_NA + 2 * D_SGU
MOE_GROUPS = 4
MOE_EXPERTS_PER_GROUP = 4
MOE_EXPERTS = MOE_GROUPS * MOE_EXPERTS_PER_GROUP
MOE_TOP_K = 2
D_EXPERT = 256
DEEPNORM_ALPHA = (2 * DEPTH) ** 0.25
DEEPNORM_BETA = (8 * DEPTH) ** -0.25
LN_EPS = 1e-5

kernel_name = "hybrid_natten_sgu_hmoe_deepnorm"


def layer_norm(x, g, b):
    xf = x.astype(jnp.float32)
    mu = jnp.mean(xf, axis=-1, keepdims=True)
    xc = xf - mu
    var = jnp.mean(jnp.square(xc), axis=-1, keepdims=True)
    y = xc * lax.rsqrt(var + LN_EPS) * g.astype(jnp.float32) + b.astype(jnp.float32)
    return y.astype(x.dtype)


def rms_norm(x, g):
    xf = x.astype(jnp.float32)
    y = xf * lax.rsqrt(jnp.mean(jnp.square(xf), axis=-1, keepdims=True) + LN_EPS)
    return (y * g.astype(jnp.float32)).astype(x.dtype)


def neighbourhood_attention(q, k, v, rel_bias):
    B, T, H, Dh = q.shape
    rows = T // GRID_W
    kh = min(NA_WIN_ROWS, rows)
    kw = NA_WIN_COLS
    qg = q.reshape(B, rows, GRID_W, H, Dh)
    kg = k.reshape(B, rows, GRID_W, H, Dh)
    vg = v.reshape(B, rows, GRID_W, H, Dh)
    cols = np.arange(GRID_W)
    col_start = np.clip(cols - kw // 2, 0, GRID_W - kw)
    col_idx = col_start[:, None] + np.arange(kw)[None, :]
    dc = jnp.asarray(col_idx - cols[:, None], jnp.int32)
    row_start = np.clip(np.arange(rows) - kh // 2, 0, rows - kh)

    def one_row(args):
        r, rs = args
        q_r = lax.dynamic_index_in_dim(qg, r, axis=1, keepdims=False)
        k_r = lax.dynamic_slice_in_dim(kg, rs, kh, axis=1)
        v_r = lax.dynamic_slice_in_dim(vg, rs, kh, axis=1)
        k_n = k_r[:, :, col_idx]
        v_n = v_r[:, :, col_idx]
        dr = rs + jnp.arange(kh, dtype=jnp.int32) - r
        bias = rel_bias[:, dr[None, :, None] + (NA_WIN_ROWS - 1),
                        dc[:, None, :] + (NA_WIN_COLS - 1)]
        s = jnp.einsum('bchd,bicjhd->bhcij', q_r, k_n)
        logits = s.astype(jnp.float32) + bias.astype(jnp.float32)
        p = jax.nn.softmax(logits.reshape(B, H, GRID_W, kh * kw), axis=-1)
        p = p.reshape(B, H, GRID_W, kh, kw).astype(v.dtype)
        return jnp.einsum('bhcij,bicjhd->bchd', p, v_n)

    out = lax.map(one_row, (jnp.arange(rows, dtype=jnp.int32),
                            jnp.asarray(row_start, jnp.int32)))
    return jnp.transpose(out, (1, 0, 2, 3, 4)).reshape(B, T, H * Dh)


def spatial_gating(u, vs, ln_g, ln_b, w_s, b_s):
    B, T, _ = u.shape
    n_chunks = T // SGU_CHUNK
    vg = vs.reshape(B, T, SGU_GROUPS, SGU_GROUP_DIM)
    vg = layer_norm(vg, ln_g.reshape(SGU_GROUPS, SGU_GROUP_DIM), ln_b.reshape(SGU_GROUPS, SGU_GROUP_DIM))
    vg = vg.reshape(B, n_chunks, SGU_CHUNK, SGU_GROUPS, SGU_GROUP_DIM)
    mixed = jnp.einsum('gpq,bnqgd->bnpgd', w_s, vg) + jnp.transpose(b_s)[:, :, None]
    return u * mixed.reshape(B, T, D_SGU)


def hierarchical_moe(h, w_rg, b_rg, w_re, b_re, w_gate, w_up, w_down):
    B, T, D = h.shape
    hf = h.reshape(B * T, D)
    g_logits = (hf @ w_rg).astype(jnp.float32) + b_rg.astype(jnp.float32)
    p_group = jax.nn.softmax(g_logits, axis=-1)
    g_star = jnp.argmax(g_logits, axis=-1)
    gate_group = jnp.take_along_axis(p_group, g_star[:, None], axis=1)
    e_logits = ((hf @ w_re).astype(jnp.float32) + b_re.astype(jnp.float32)
                ).reshape(B * T, MOE_GROUPS, MOE_EXPERTS_PER_GROUP)
    e_sel = jnp.take_along_axis(e_logits, g_star[:, None, None], axis=1)[:, 0]
    top_vals, top_idx = lax.top_k(e_sel, MOE_TOP_K)
    top_w = jax.nn.softmax(top_vals, axis=-1)
    within = jnp.sum(jax.nn.one_hot(top_idx, MOE_EXPERTS_PER_GROUP, dtype=jnp.float32)
                     * top_w[..., None], axis=1)
    combine = (jax.nn.one_hot(g_star, MOE_GROUPS, dtype=jnp.float32)[:, :, None]
               * within[:, None, :] * gate_group[:, :, None]).reshape(B * T, MOE_EXPERTS)
    combine = combine.astype(h.dtype)
    y = jnp.zeros_like(hf)
    for e in range(MOE_EXPERTS):
        act = jax.nn.silu(hf @ w_gate[e]) * (hf @ w_up[e])
        y = y + combine[:, e:e + 1] * (act @ w_down[e])
    return y.reshape(B, T, D)


def setup_inputs(seed: int = 0) -> dict:
    key = jax.random.key(seed)
    ks = jax.random.split(key, 20)
    f32 = jnp.float32
    nrm = lambda k, shape, s: jax.random.normal(k, shape, f32) * s
    x = jax.random.normal(ks[0], (BATCH, SEQ, D_MODEL), f32)
    col_scale = jnp.concatenate([jnp.ones((2 * D_NA,), f32),
                                 jnp.full((D_NA,), DEEPNORM_BETA, f32),
                                 jnp.ones((2 * D_SGU,), f32)])
    w_in = nrm(ks[1], (DEPTH, D_MODEL, D_IN), D_MODEL ** -0.5) * col_scale
    w_out = nrm(ks[2], (DEPTH, D_MIX, D_MODEL), DEEPNORM_BETA * D_MIX ** -0.5)
    na_rel_bias = nrm(ks[3], (DEPTH, NA_HEADS, 2 * NA_WIN_ROWS - 1, 2 * NA_WIN_COLS - 1), 0.02)
    sgu_ln_g = 1.0 + nrm(ks[4], (DEPTH, D_SGU), 0.02)
    sgu_ln_b = nrm(ks[5], (DEPTH, D_SGU), 0.02)
    sgu_w = nrm(ks[6], (DEPTH, SGU_GROUPS, SGU_CHUNK, SGU_CHUNK), SGU_CHUNK ** -0.5)
    sgu_b = 1.0 + nrm(ks[7], (DEPTH, SGU_GROUPS, SGU_CHUNK), 0.02)
    mix_norm_g = 1.0 + nrm(ks[8], (DEPTH, D_MIX), 0.02)
    ln1_g = 1.0 + nrm(ks[9], (DEPTH, D_MODEL), 0.02)
    ln1_b = nrm(ks[10], (DEPTH, D_MODEL), 0.02)
    router_group_w = nrm(ks[11], (DEPTH, D_MODEL, MOE_GROUPS), D_MODEL ** -0.5)
    router_group_b = nrm(ks[12], (DEPTH, MOE_GROUPS), 0.01)
    router_expert_w = nrm(ks[13], (DEPTH, D_MODEL, MOE_EXPERTS), D_MODEL ** -0.5)
    router_expert_b = nrm(ks[14], (DEPTH, MOE_EXPERTS), 0.01)
    expert_w_gate = nrm(ks[15], (DEPTH, MOE_EXPERTS, D_MODEL, D_EXPERT), DEEPNORM_BETA * D_MODEL ** -0.5)
    expert_w_up = nrm(ks[16], (DEPTH, MOE_EXPERTS, D_MODEL, D_EXPERT), DEEPNORM_BETA * D_MODEL ** -0.5)
    expert_w_down = nrm(ks[17], (DEPTH, MOE_EXPERTS, D_EXPERT, D_MODEL), DEEPNORM_BETA * D_EXPERT ** -0.5)
    ln2_g = 1.0 + nrm(ks[18], (DEPTH, D_MODEL), 0.02)
    ln2_b = nrm(ks[19], (DEPTH, D_MODEL), 0.02)
    return {"x": x, "w_in": w_in, "w_out": w_out, "na_rel_bias": na_rel_bias,
            "sgu_ln_g": sgu_ln_g, "sgu_ln_b": sgu_ln_b, "sgu_w": sgu_w, "sgu_b": sgu_b,
            "mix_norm_g": mix_norm_g, "ln1_g": ln1_g, "ln1_b": ln1_b,
            "router_group_w": router_group_w, "router_group_b": router_group_b,
            "router_expert_w": router_expert_w, "router_expert_b": router_expert_b,
            "expert_w_gate": expert_w_gate, "expert_w_up": expert_w_up,
            "expert_w_down": expert_w_down, "ln2_g": ln2_g, "ln2_b": ln2_b}


def reference(x, w_in, w_out, na_rel_bias, sgu_ln_g, sgu_ln_b, sgu_w, sgu_b, mix_norm_g,
              ln1_g, ln1_b, router_group_w, router_group_b, router_expert_w, router_expert_b,
              expert_w_gate, expert_w_up, expert_w_down, ln2_g, ln2_b):
    B, T, _ = x.shape
    splits = [D_NA, 2 * D_NA, 3 * D_NA, 3 * D_NA + D_SGU]
    for l in range(DEPTH):
        proj = jnp.einsum('btd,de->bte', x, w_in[l])
        q, k, v, u, vs = jnp.split(proj, splits, axis=-1)
        q = q.reshape(B, T, NA_HEADS, NA_HEAD_DIM) * (NA_HEAD_DIM ** -0.5)
        k = k.reshape(B, T, NA_HEADS, NA_HEAD_DIM)
        v = v.reshape(B, T, NA_HEADS, NA_HEAD_DIM)
        a_out = neighbourhood_attention(q, k, v, na_rel_bias[l])
        s_out = spatial_gating(jax.nn.gelu(u), jax.nn.gelu(vs), sgu_ln_g[l], sgu_ln_b[l],
                               sgu_w[l], sgu_b[l])
        mixed = jnp.concatenate([rms_norm(a_out, mix_norm_g[l, :D_NA]),
                                 rms_norm(s_out, mix_norm_g[l, D_NA:])], axis=-1)
        mix_out = jnp.einsum('bte,ed->btd', mixed, w_out[l])
        x = layer_norm(DEEPNORM_ALPHA * x + mix_out, ln1_g[l], ln1_b[l])
        moe_out = hierarchical_moe(x, router_group_w[l], router_group_b[l],
                                   router_expert_w[l], router_expert_b[l],
                                   expert_w_gate[l], expert_w_up[l], expert_w_down[l])
        x = layer_norm(DEEPNORM_ALPHA * x + moe_out, ln2_g[l], ln2_b[l])
    return x
```

```python
import contextlib
import numpy as np
import concourse.bass as bass
import concourse.mybir as mybir
from concourse.bass_utils import run_bass_kernel_spmd

F32 = mybir.dt.float32
BF16 = mybir.dt.bfloat16
AF = mybir.ActivationFunctionType
ALU = mybir.AluOpType
AX = mybir.AxisListType

ALPHA = float(8 ** 0.25)
EPS = 1e-5
NEG = -30000.0
ENGS = ("pe", "act", "dve", "pool", "sp")


class Sched:
    def __init__(self, nc):
        self.nc = nc
        self.ops = []
        self.last_writer = {}
        self.readers = {}

    def capture_begin(self):
        self._cap = []

    def capture_end(self):
        c, self._cap = self._cap, None
        return c

    def commit_interleaved(self, lists, bias=None):
        has0 = bool(lists) and bool(lists[0])
        if bias is None:
            bias = [0.0] * len(lists)
        bias = [b for l, b in zip(lists, bias) if l]
        lists = [l for l in lists if l]
        pos = [0] * len(lists)
        while True:
            best, bf = None, None
            for i, l in enumerate(lists):
                if pos[i] < len(l):
                    f = pos[i] / len(l) - bias[i]
                    if bf is None or f < bf:
                        best, bf = i, f
            if best is None:
                break
            item = lists[best][pos[best]]
            if has0 and item[6] is not None and best != 0 and pos[0] < min(item[6], len(lists[0])):
                best = 0
                item = lists[0][pos[0]]
            self.add(*item[:6])
            pos[best] += 1
            while len(item) > 7 and item[7] and pos[best] < len(lists[best]):
                item = lists[best][pos[best]]
                self.add(*item[:6])
                pos[best] += 1

    def add(self, eng, fn, reads=(), writes=(), kind="c", dsem=None, req=None, glue=False):
        if getattr(self, "_cap", None) is not None:
            self._cap.append((eng, fn, tuple(reads), tuple(writes), kind, dsem, req, glue))
            return -1
        idx = len(self.ops)
        deps = {}
        for r in reads:
            w = self.last_writer.get(r)
            if w is not None:
                deps[w] = "raw"
        for wr in writes:
            w = self.last_writer.get(wr)
            if w is not None and w not in deps:
                deps[w] = "waw"
            rd = self.readers.get(wr)
            if rd:
                for j in rd[0].values():
                    if j not in deps:
                        deps[j] = "war"
                for j in rd[1]:
                    if j not in deps:
                        deps[j] = "war"
        for r in reads:
            rd = self.readers.setdefault(r, ({}, []))
            if kind != "c":
                rd[1].append(idx)
            else:
                rd[0][eng] = idx
        for wr in writes:
            self.last_writer[wr] = idx
            self.readers[wr] = ({}, [])
        self.ops.append(dict(eng=eng, fn=fn, deps=deps, kind=kind, dsem=dsem,
                             signal=False, sigval=None))
        return idx

    def emit(self, sems, dsems, final_wait=()):
        ops = self.ops
        need = [[] for _ in ops]
        for i, op in enumerate(ops):
            for j, typ in op["deps"].items():
                src = ops[j]
                if src["kind"] == "c" and op["kind"] == "c" and src["eng"] == op["eng"]:
                    if op["eng"] == "pe" or typ != "raw":
                        continue
                need[i].append(j)
                src["signal"] = True
        cnt = {e: 0 for e in ENGS}
        dcnt = {k: 0 for k in dsems}
        for op in ops:
            if op["kind"] in ("d", "cc"):
                dcnt[op["dsem"]] += 16 if op["kind"] == "d" else 1
                op["sigval"] = (op["dsem"], dcnt[op["dsem"]])
            elif op["signal"]:
                cnt[op["eng"]] += 1
                op["sigval"] = (op["eng"], cnt[op["eng"]])

        def run(engname, eng):
            waited = {}
            for i, op in enumerate(ops):
                if op["eng"] != engname:
                    continue
                req = {}
                for j in need[i]:
                    k, v = ops[j]["sigval"]
                    if v > req.get(k, 0):
                        req[k] = v
                for k, v in req.items():
                    if waited.get(k, 0) >= v:
                        continue
                    waited[k] = v
                    eng.wait_ge(sems[k] if k in sems else dsems[k], v)
                ins = op["fn"](eng)
                if op["kind"] == "d":
                    ins.then_inc(dsems[op["dsem"]], 16)
                elif op["kind"] == "cc":
                    ins.then_inc(dsems[op["dsem"]], 1)
                elif op["signal"]:
                    ins.then_inc(sems[op["eng"]], 1)
            if engname == "sp":
                for k in final_wait:
                    eng.wait_ge(dsems[k], dcnt[k])

        with self.nc.Block() as block:
            @block.tensor
            def _(e):
                run("pe", e)

            @block.scalar
            def _(e):
                run("act", e)

            @block.vector
            def _(e):
                run("dve", e)

            @block.gpsimd
            def _(e):
                run("pool", e)

            @block.sync
            def _(e):
                run("sp", e)


N_OWN = 16
N_Q = 8


def build_program(depth=4, use_cc=True):
    n_q = N_Q
    NT = n_q * 128
    RK = 8
    RQ = 4
    nc = bass.Bass("TRN2", target_bir_lowering=False)

    def din(name, shape):
        return nc.dram_tensor(name, shape, F32, kind="ExternalInput").ap()

    L = depth
    x_d = din("x", [(N_OWN + 2) * 128, 1024])
    w_in_d = din("w_in", [L, 1024, 2560])
    w_out_d = din("w_out", [L, 1024, 1024])
    bm_int_d = din("bm_int", [L, 128, 5 * 1024])
    bm_edge_d = din("bm_edge", [L, 16, 128, 512])
    wsT_d = din("wsT", [L, 128, 1024])
    bs_d = din("bs", [L, 128, 8])
    sgug_d = din("sgu_g", [L, 1, 512])
    sgub_d = din("sgu_b", [L, 1, 512])
    mixg_d = din("mixg", [L, 128, 8])
    ln1g_d = din("ln1g", [L, 1, 1024])
    ln1b_d = din("ln1b", [L, 1, 1024])
    ln2g_d = din("ln2g", [L, 1, 1024])
    ln2b_d = din("ln2b", [L, 1, 1024])
    wr_d = din("wr", [L, 1024, 20])
    br_d = din("br", [L, 1, 20])
    wg_d = din("wg", [L, 16, 1024, 256])
    wu_d = din("wu", [L, 16, 1024, 256])
    wd_d = din("wd", [L, 16, 256, 1024])
    ident_d = din("ident", [128, 128])
    jmat_d = din("jmat", [128, 128])
    sel_d = din("sel", [128, 2])
    out_d = nc.dram_tensor("out", [N_OWN * 128, 1024], F32, kind="ExternalOutput").ap()
    bufs = [nc.dram_tensor("xbuf%d" % i, [N_OWN * 128, 1024], F32).ap() for i in range(2)]
    cc_in = nc.dram_tensor("cc_in", [256, 1024], F32)
    cc_out = nc.dram_tensor("cc_out", [512, 1024], F32)

    with contextlib.ExitStack() as es:
        def sb(name, shape, dt=F32):
            return es.enter_context(nc.sbuf_tensor("sb_" + name, shape, dt))

        def ps(name, shape, dt=F32):
            return es.enter_context(nc.psum_tensor("ps_" + name, shape, dt))

        x_res = sb("x_res", [128, n_q, 1024])
        w_in = sb("w_in", [128, 8, 2560], BF16)
        w_out = sb("w_out", [128, 8, 1024], BF16)
        E_int = sb("E_int", [128, 5, 8, 128], BF16)
        est = [sb("est0", [128, 512])] * 2
        eed = [sb("eed0", [128, 512], BF16)] * 2
        xst = [sb("xst%d" % i, [128, 1024]) for i in range(2)]
        xb = [sb("xb%d" % i, [128, 1024], BF16) for i in range(2)]
        xT = [sb("xT%d" % i, [128, 8, 128], BF16) for i in range(3)]
        arena = sb("arena", [128, 15360], BF16)
        kT = arena[:, 0:4096].rearrange("p (r c n) -> p r c n", r=RK, c=4)
        Vt = arena[:, 4096:8256].rearrange("p (r h d) -> p r h d", r=RK, h=8)
        qT = arena[:, 8256:10304].rearrange("p (r c n) -> p r c n", r=RQ, c=4)
        gu_ = [sb("gu%d" % i, [128, 512]) for i in range(2)]
        gv_ = [sb("gv%d" % i, [128, 512]) for i in range(2)]
        sq = sb("sq", [128, 512])
        vn = sb("vn", [128, 512], BF16)
        mixed = arena[:, 10304:14400].rearrange("p (r n) -> p r n", r=RQ)
        pexp = [sb("pexp%d" % i, [128, 640], BF16) for i in range(2)]
        pTs = [sb("pTs%d" % i, [128, 640], BF16) for i in range(2)]
        a_o = sb("a_o", [128, 512])
        mT = sb("mT", [128, 8, 128], BF16)
        kb_ = sb("kb_", [128, 512], BF16)
        qb_ = sb("qb_", [128, 512], BF16)
        x1T = sb("x1T", [128, 8, NT], BF16)
        ln1g = sb("ln1g", [128, 1024])
        ln1b = sb("ln1b", [128, 1024])
        sgug = sb("sgug", [128, 512])
        sgub = sb("sgub", [128, 512])
        wsT = sb("wsT", [128, 8, 128], BF16)
        wr = sb("wr", [128, 8, 20], BF16)
        br = sb("br", [128, 20])
        bs = sb("bs", [128, 8])
        mixg = sb("mixg", [128, 8])
        ident = sb("ident", [128, 128], BF16)
        jmat = sb("jmat", [128, 128], BF16)
        sel = sb("sel", [128, 2])
        nh = sb("nh", [128, 8])
        comb = sb("comb", [128, n_q, 16])
        sm = sb("sm", [128, 192])
        junk = sb("junk", [128, 512], BF16)
        xbc = sb("xbc", [128, 1024], BF16)
        rscr = [sb("rscr%d" % i, [128, 96]) for i in range(2)]
        wg_b, wu_b, wd_b = [], [], []
        for i in range(2):
            o = i * 6144
            wg_b.append(arena[:, o:o + 2048].rearrange("p (k f) -> p k f", k=8))
            wu_b.append(arena[:, o + 2048:o + 4096].rearrange("p (k f) -> p k f", k=8))
            wd_b.append(arena[:, o + 4096:o + 6144].rearrange("p (c n) -> p c n", c=2))
        sg_b = [arena[:, 12288 + i * 512:12288 + (i + 1) * 512].rearrange("p (c n) -> p c n", c=1) for i in range(2)]
        act_b = [arena[:, 13312 + i * 1024:13312 + (i + 1) * 1024].rearrange("p (c n) -> p c n", c=2) for i in range(2)]

        B = [ps("B%d" % i, [128, 512]) for i in range(8)]
        pTv = [B[i][:].bitcast(BF16).rearrange("p (k n) -> p k n", k=8) for i in range(8)]

        sems = {e: es.enter_context(nc.semaphore("s_" + e)) for e in ENGS}
        dnames = (["c%d" % i for i in range(16)] + ["xl%d" % i for i in range(12)] + ["xw%d" % i for i in range(18)]
                  + ["ee0", "ee1", "ewg0", "ewg1", "ewu0", "ewu1", "ewd0", "ewd1", "st", "win0", "win1", "win2", "win3", "ei0", "ei1", "cc0", "cc1", "cc2", "cc3", "ga", "gb"])
        dsems = {k: es.enter_context(nc.semaphore("d_" + k)) for k in dnames}
        S = Sched(nc)
        A = S.add

        W_IN_TOK = ["w_in_l%d" % i for i in range(4)]
        E_INT_TOK = ["E_int%d" % s for s in range(5)]
        EXP_TOK = ["wg0", "wu0", "wd0", "wg1", "wu1", "wd1"]
        RING_TOK = (["kT%d" % i for i in range(RK)] + ["V%d" % i for i in range(RK)] + ["qT%d" % i for i in range(RQ)]
                    + ["mixA%d" % i for i in range(RQ)] + ["mixS%d" % i for i in range(RQ)] + ["Vones"])
        SGACT_TOK = ["sg%d_%d" % (i, f) for i in range(2) for f in range(2)] + ["act%d_%d" % (i, f) for i in range(2) for f in range(2)]

        A("pool", lambda e: e.memset(nh[:], -0.5), writes=["nh"])
        ident_f = xst[0][:, 0:128]
        A("sp", lambda e: e.dma_start(out=ident_f, in_=ident_d), writes=["xst0"], kind="d", dsem="c0")
        A("dve", lambda e: e.tensor_copy(out=ident[:], in_=ident_f), reads=["xst0"], writes=["ident"])
        A("sp", lambda e: e.dma_start(out=ident_f, in_=jmat_d), reads=["ident"], writes=["xst0"], kind="d", dsem="c13")
        A("dve", lambda e: e.tensor_copy(out=jmat[:], in_=ident_f), reads=["xst0"], writes=["jmat"])
        A("sp", lambda e: e.dma_start(out=sel[:], in_=sel_d), writes=["sel"], kind="d", dsem="c14")

        def ln_tile(m, g_ap, b_ap, gtoks, eps=EPS):
            xr = x_res[:, m, :]
            t = "xres%d" % m
            o = 128 * (m % 2)
            sfx = "_%d" % (m % 2)
            A("dve", lambda e: e.bn_stats(out=sm[:, o:o + 6], in_=xr[:, 0:512]), reads=[t], writes=["st0" + sfx])
            A("dve", lambda e: e.bn_stats(out=sm[:, o + 6:o + 12], in_=xr[:, 512:1024]), reads=[t], writes=["st1" + sfx])
            A("dve", lambda e: e.bn_aggr(out=sm[:, o + 12:o + 14], in_=sm[:, o:o + 12]), reads=["st0" + sfx, "st1" + sfx], writes=["mv" + sfx])
            A("dve", lambda e: e.tensor_scalar(out=sm[:, o + 14:o + 15], in0=sm[:, o + 13:o + 14], scalar1=eps, scalar2=None, op0=ALU.add),
              reads=["mv" + sfx], writes=["ve" + sfx])
            A("pool", lambda e: e.tensor_tensor(out=sm[:, o + 15:o + 16], in0=sm[:, o + 14:o + 15], in1=nh[:, 0:1], op=ALU.pow),
              reads=["ve" + sfx, "nh"], writes=["rstd" + sfx])
            A("dve", lambda e: e.scalar_tensor_tensor(out=xr, in0=xr, scalar=sm[:, o + 12:o + 13], in1=g_ap, op0=ALU.subtract, op1=ALU.mult),
              reads=[t, "mv" + sfx, gtoks[0]], writes=[t])
            A("dve", lambda e: e.scalar_tensor_tensor(out=xr, in0=xr, scalar=sm[:, o + 15:o + 16], in1=b_ap, op0=ALU.mult, op1=ALU.add),
              reads=[t, "rstd" + sfx, gtoks[1]], writes=[t])

        def rms_to(src_ap, src_tok, dst_ap, dst_tok, col, jk):
            A("act", lambda e: e.activation(out=jk, in_=src_ap, func=AF.Square, accum_out=sm[:, col:col + 1]),
              reads=list(src_tok), writes=["ss%d" % col])
            A("dve", lambda e: e.tensor_scalar(out=sm[:, col + 1:col + 2], in0=sm[:, col:col + 1], scalar1=1.0 / 512, scalar2=EPS,
                                               op0=ALU.mult, op1=ALU.add), reads=["ss%d" % col], writes=["sv%d" % col])
            A("pool", lambda e: e.tensor_tensor(out=sm[:, col + 2:col + 3], in0=sm[:, col + 1:col + 2], in1=nh[:, 0:1], op=ALU.pow),
              reads=["sv%d" % col, "nh"], writes=["sr%d" % col])
            A("dve", lambda e: e.tensor_scalar(out=dst_ap, in0=src_ap, scalar1=sm[:, col + 2:col + 3], scalar2=None, op0=ALU.mult),
              reads=list(src_tok) + ["sr%d" % col], writes=[dst_tok])

        def transpose_to(src_ap, src_tok, evac, bank):
            for k in range(8):
                A("pe", lambda e, k=k: e.transpose(out=pTv[bank][:, k, :], in_=src_ap[:, k * 128:(k + 1) * 128], identity=ident[:]),
                  reads=list(src_tok) + ["ident"], writes=["B%d" % bank], glue=True)
            evac(pTv[bank], "B%d" % bank)

        def layer_consts(l):
            A("pool", lambda e: e.dma_start(out=wsT[:], in_=wsT_d[l].rearrange("p (g q) -> p g q", g=8)), writes=["wsT"], kind="d", dsem="c1")
            A("pool", lambda e: e.dma_start(out=wr[:], in_=wr_d[l].rearrange("(k p) n -> p k n", p=128)), writes=["wr"], kind="d", dsem="c2")
            A("sp", lambda e: e.dma_start(out=bs[:], in_=bs_d[l]), writes=["bs"], kind="d", dsem="c4")
            A("sp", lambda e: e.dma_start(out=mixg[:], in_=mixg_d[l]), writes=["mixg"], kind="d", dsem="c5")
            A("sp", lambda e: e.dma_start(out=sgug[:], in_=sgug_d[l].partition_broadcast(128)), writes=["sgug"], kind="d", dsem="c6")
            A("sp", lambda e: e.dma_start(out=sgub[:], in_=sgub_d[l].partition_broadcast(128)), writes=["sgub"], kind="d", dsem="c7")
            if l == 0:
                A("sp", lambda e: e.dma_start(out=ln1g[:], in_=ln1g_d[l].partition_broadcast(128)), writes=["ln1g"], kind="d", dsem="c8")
                A("sp", lambda e: e.dma_start(out=ln1b[:], in_=ln1b_d[l].partition_broadcast(128)), writes=["ln1b"], kind="d", dsem="c9")
            A("sp", lambda e: e.dma_start(out=br[:], in_=br_d[l].partition_broadcast(128)), writes=["br"], kind="d", dsem="c10")
            for s in range(5):
                A("sp", lambda e, s=s: e.dma_start(out=xst[s % 2][:], in_=bm_int_d[l, :, s * 1024:(s + 1) * 1024]),
                  writes=["xst%d" % (s % 2)], kind="d", dsem="ei%d" % (s % 2))
                A("act", lambda e, s=s: e.activation(out=E_int[:, s, :, :].rearrange("p h q -> p (h q)"), in_=xst[s % 2][:], func=AF.Exp),
                  reads=["xst%d" % (s % 2)], writes=["E_int%d" % s])

        def mixer_weight_chunk(l, q4):
            if q4 < 4:
                A("pool", lambda e: e.dma_start(out=w_in[:, 2 * q4:2 * q4 + 2, :],
                                                in_=w_in_d[l, q4 * 256:(q4 + 1) * 256, :].rearrange("(k p) n -> p k n", p=128)),
                  writes=["w_in_l%d" % q4], kind="d", dsem="win%d" % q4)
            else:
                A("pool", lambda e: e.dma_start(out=w_out[:], in_=w_out_d[l].rearrange("(k p) n -> p k n", p=128)),
                  writes=["w_out"], kind="d", dsem="c3")

        def arena_barrier():
            A("pool", lambda e: e.memset(sm[:, 127:128], 0.0), writes=RING_TOK + EXP_TOK + SGACT_TOK)

        def subblock(l, sbk, src_tile_ap, dst_tile_ap, dst_tok, last_layer, prefetch, pending_ln2):
            if sbk == 0:
                kv_tiles = list(range(0, 10))
                q_off = 0
            else:
                kv_tiles = list(range(6, 18))
                q_off = 2
            n_kv = len(kv_tiles)

            def win(m):
                if sbk == 0:
                    return [0, 1, 2, 3] if m < 2 else list(range(m - 2, m + 3))
                return list(range(m, m + 5))

            in_prologue = [True]
            S.capture_begin()
            arena_barrier()
            A("pool", lambda e: e.memset(Vt[:, :, :, 64:65], 1.0), writes=["Vones"])

            def A_kv(j):
                tile = kv_tiles[j]
                m = j - q_off
                own = 0 <= m < n_q
                r2 = j % 2
                r3 = j % 3
                rk = j % RK
                rev = (l > 0 and tile >= N_OWN)
                if own:
                    xsrc, xtok = x_res[:, m, :], "xres%d" % m
                else:
                    xsrc, xtok = xst[r2][:], "xst%d" % r2
                if not rev:
                    sap, stok = src_tile_ap(tile)
                    rq_ = (pending_ln2[1][m] if (own and in_prologue[0] and pending_ln2 is not None) else None)
                    A("sp", lambda e: e.dma_start(out=xsrc, in_=sap), reads=[stok], writes=[xtok], kind="d", dsem="xl%d" % j, req=rq_)
                    A("act", lambda e: e.copy(out=xb[r2][:], in_=xsrc), reads=[xtok], writes=["xb%d" % r2])
                    transpose_to(xb[r2][:], ["xb%d" % r2],
                                 lambda pv, tk: A("act", lambda e: e.copy(out=xT[r3][:], in_=pv), reads=[tk], writes=["xT%d" % r3]), 1)
                else:
                    o = (17 - tile) * 128
                    cco = cc_out.ap()
                    A("sp", lambda e: e.dma_start(out=xst[0][:], in_=cco[o:o + 128, :]), reads=["gath"], writes=["xst0"], kind="d", dsem="ga")
                    A("sp", lambda e: e.dma_start(out=xst[1][:], in_=cco[256 + o:256 + o + 128, :]), reads=["gath"], writes=["xst1"], kind="d", dsem="gb")
                    A("dve", lambda e: e.tensor_scalar(out=xst[0][:], in0=xst[0][:], scalar1=sel[:, 0:1], scalar2=None, op0=ALU.mult),
                      reads=["xst0", "sel"], writes=["xst0"])
                    A("dve", lambda e: e.scalar_tensor_tensor(out=xst[0][:], in0=xst[1][:], scalar=sel[:, 1:2], in1=xst[0][:],
                                                              op0=ALU.mult, op1=ALU.add), reads=["xst0", "xst1", "sel"], writes=["xst0"])
                    A("act", lambda e: e.copy(out=xb[r2][:], in_=xst[0][:]), reads=["xst0"], writes=["xb%d" % r2])
                    for hf in range(2):
                        for k4 in range(4):
                            k = hf * 4 + k4
                            A("pe", lambda e, k=k, k4=k4: e.matmul(B[1][:, k4 * 128:(k4 + 1) * 128], lhsT=xb[r2][:, k * 128:(k + 1) * 128],
                                                                   rhs=jmat[:], start=True, stop=True), reads=["xb%d" % r2, "jmat"], writes=["B1"])
                        A("act", lambda e, hf=hf: e.copy(out=xT[r3][:, hf * 4:(hf + 1) * 4, :], in_=B[1][:].rearrange("p (k t) -> p k t", k=4)),
                          reads=["B1"], writes=["xT%d_%d" % (r3, hf)])
                xt = xT[r3]
                xtt = ["xT%d" % r3, "xT%d_0" % r3, "xT%d_1" % r3]
                for k in range(8):
                    A("pe", lambda e, k=k: e.matmul(B[1][:], lhsT=xt[:, k, :], rhs=w_in[:, k, 512:1024], start=(k == 0), stop=(k == 7)),
                      reads=xtt + W_IN_TOK, writes=["B1"])
                A("act", lambda e: e.copy(out=kb_[:], in_=B[1][:]), reads=["B1"], writes=["kb_"])
                for c in range(4):
                    A("pe", lambda e, c=c: e.transpose(out=pTv[1][:, c, :], in_=kb_[:, c * 128:(c + 1) * 128], identity=ident[:]),
                      reads=["kb_", "ident"], writes=["B1"])
                A("act", lambda e: e.copy(out=kT[:, rk, :, :], in_=pTv[1][:, 0:4, :]), reads=["B1"], writes=["kT%d" % rk])
                for k in range(8):
                    A("pe", lambda e, k=k: e.matmul(B[1][:], lhsT=xt[:, k, :], rhs=w_in[:, k, 1024:1536], start=(k == 0), stop=(k == 7)),
                      reads=xtt + W_IN_TOK, writes=["B1"])
                A("act", lambda e: e.copy(out=Vt[:, rk, :, 0:64], in_=B[1][:].rearrange("p (h d) -> p h d", h=8)),
                  reads=["B1", "Vones"], writes=["V%d" % rk])

            def A_q(m):
                j = m + q_off
                r3 = j % 3
                xt = xT[r3]
                xtt = ["xT%d" % r3, "xT%d_0" % r3, "xT%d_1" % r3]
                rq = m % RQ
                gu, gv = gu_[m % 2], gv_[m % 2]
                for k in range(8):
                    A("pe", lambda e, k=k: e.matmul(B[2][:], lhsT=xt[:, k, :], rhs=w_in[:, k, 0:512], start=(k == 0), stop=(k == 7)),
                      reads=xtt + W_IN_TOK, writes=["B2"])
                A("act", lambda e: e.activation(out=qb_[:], in_=B[2][:], func=AF.Copy, scale=0.125), reads=["B2"], writes=["qb_"])
                for c in range(4):
                    A("pe", lambda e, c=c: e.transpose(out=pTv[2][:, c, :], in_=qb_[:, c * 128:(c + 1) * 128], identity=ident[:]),
                      reads=["qb_", "ident"], writes=["B2"])
                A("act", lambda e: e.copy(out=qT[:, rq, :, :], in_=pTv[2][:, 0:4, :]), reads=["B2"], writes=["qT%d" % rq])
                for k in range(8):
                    A("pe", lambda e, k=k: e.matmul(B[2][:], lhsT=xt[:, k, :], rhs=w_in[:, k, 1536:2048], start=(k == 0), stop=(k == 7)),
                      reads=xtt + W_IN_TOK, writes=["B2"])
                A("act", lambda e: e.activation(out=gu[:], in_=B[2][:], func=AF.Gelu_apprx_tanh), reads=["B2"], writes=["gu%d" % (m % 2)])
                for k in range(8):
                    A("pe", lambda e, k=k: e.matmul(B[2][:], lhsT=xt[:, k, :], rhs=w_in[:, k, 2048:2560], start=(k == 0), stop=(k == 7)),
                      reads=xtt + W_IN_TOK, writes=["B2"])
                A("act", lambda e: e.activation(out=gv[:], in_=B[2][:], func=AF.Gelu_apprx_tanh), reads=["B2"], writes=["gv%d" % (m % 2)])

            def sgu_chain(m):
                rq = m % RQ
                gu, gv = gu_[m % 2], gv_[m % 2]
                tgv, tgu = "gv%d" % (m % 2), "gu%d" % (m % 2)
                gv3 = gv[:].rearrange("p (g d) -> p g d", g=8)
                sq3 = sq[:].rearrange("p (g d) -> p g d", g=8)
                A("dve", lambda e: e.tensor_reduce(out=sm[:, 32:40], in_=gv3, axis=AX.X, op=ALU.add), reads=[tgv], writes=["s1"])
                A("dve", lambda e: e.tensor_tensor(out=sq[:], in0=gv[:], in1=gv[:], op=ALU.mult), reads=[tgv], writes=["sq"])
                A("dve", lambda e: e.tensor_reduce(out=sm[:, 40:48], in_=sq3, axis=AX.X, op=ALU.add), reads=["sq"], writes=["s2"])
                A("dve", lambda e: e.tensor_scalar(out=sm[:, 48:56], in0=sm[:, 32:40], scalar1=1.0 / 64, scalar2=None, op0=ALU.mult),
                  reads=["s1"], writes=["gmean"])
                A("dve", lambda e: e.tensor_tensor(out=sm[:, 56:64], in0=sm[:, 48:56], in1=sm[:, 48:56], op=ALU.mult),
                  reads=["gmean"], writes=["gmsq"])
                A("dve", lambda e: e.scalar_tensor_tensor(out=sm[:, 64:72], in0=sm[:, 40:48], scalar=1.0 / 64, in1=sm[:, 56:64],
                                                          op0=ALU.mult, op1=ALU.subtract), reads=["s2", "gmsq"], writes=["gvar"])
                A("dve", lambda e: e.tensor_scalar(out=sm[:, 72:80], in0=sm[:, 64:72], scalar1=EPS, scalar2=None, op0=ALU.add),
                  reads=["gvar"], writes=["gve"])
                A("pool", lambda e: e.tensor_tensor(out=sm[:, 80:88], in0=sm[:, 72:80], in1=nh[:], op=ALU.pow),
                  reads=["gve", "nh"], writes=["grstd"])
                A("dve", lambda e: e.tensor_tensor(out=gv3, in0=gv3, in1=sm[:, 48:56].unsqueeze(2).to_broadcast([128, 8, 64]), op=ALU.subtract),
                  reads=[tgv, "gmean"], writes=[tgv])
                A("dve", lambda e: e.tensor_tensor(out=gv3, in0=gv3, in1=sm[:, 80:88].unsqueeze(2).to_broadcast([128, 8, 64]), op=ALU.mult),
                  reads=[tgv, "grstd"], writes=[tgv])
                A("dve", lambda e: e.tensor_tensor(out=gv[:], in0=gv[:], in1=sgug[:], op=ALU.mult), reads=[tgv, "sgug"], writes=[tgv])
                A("dve", lambda e: e.tensor_tensor(out=vn[:], in0=gv[:], in1=sgub[:], op=ALU.add), reads=[tgv, "sgub"], writes=["vn"])
                for g in range(8):
                    A("pe", lambda e, g=g: e.matmul(B[0][:, g * 64:(g + 1) * 64], lhsT=wsT[:, g, :], rhs=vn[:, g * 64:(g + 1) * 64],
                                                    start=True, stop=True), reads=["vn", "wsT"], writes=["B0"])
                A("dve", lambda e: e.tensor_tensor(out=gv3, in0=B[0][:].rearrange("p (g d) -> p g d", g=8),
                                                   in1=bs[:].unsqueeze(2).to_broadcast([128, 8, 64]), op=ALU.add),
                  reads=["B0", "bs", "vn"], writes=[tgv])
                A("dve", lambda e: e.tensor_tensor(out=gv[:], in0=gv[:], in1=gu[:], op=ALU.mult), reads=[tgv, tgu], writes=[tgv])
                rms_to(gv[:], [tgv], mixed[:, rq, 512:1024], "mixS%d" % rq, 88, junk[:])

            def step_B(m):
                rq = m % RQ
                W = win(m)
                nW = len(W)
                edge = (sbk == 0 and m < 2)
                po = [B[6][:, 0:260].rearrange("p (h d) -> p h d", h=4), B[7][:, 0:260].rearrange("p (h d) -> p h d", h=4)]

                def scores(h, part):
                    c, pb = h // 2, (h % 2) * 64
                    for i, kvj in enumerate(W):
                        if (i < 4) != (part == 0):
                            continue
                        if i < 4:
                            oap, tok = B[4][:, i * 128:(i + 1) * 128], "B4"
                        else:
                            oap, tok = B[5][:, 0:128], "B5"
                        A("pe", lambda e, oap=oap, kvj=kvj, pb=pb, c=c: e.matmul(
                            oap, lhsT=kT[pb:pb + 64, kvj % RK, c, :], rhs=qT[pb:pb + 64, rq, c, :], start=True, stop=True),
                          reads=["kT%d" % (kvj % RK), "qT%d" % rq], writes=[tok])

                def softmax_pv(h, mid):
                    hp = h % 2
                    mb = 4
                    ei = 0
                    if edge:
                        idx = m * 8 + h
                        ei = idx % 2
                        A("sp", lambda e, idx=idx, ei=ei: e.dma_start(out=est[ei][:], in_=bm_edge_d[l, idx]), writes=["est0"],
                          kind="d", dsem="ee0")
                        A("act", lambda e, ei=ei: e.activation(out=eed[ei][:], in_=est[ei][:], func=AF.Exp), reads=["est0"],
                          writes=["eed0"])
                    A("act", lambda e, hp=hp, mb=mb: e.activation(out=pexp[hp][:, 0:512], in_=B[mb][:], func=AF.Exp), reads=["B%d" % mb],
                      writes=["pexpa%d" % hp])
                    ptoks = ["pexpa%d" % hp]
                    if nW > 4:
                        A("act", lambda e, hp=hp: e.activation(out=pexp[hp][:, 512:640], in_=B[5][:, 0:128], func=AF.Exp),
                          reads=["B5"], writes=["pexpb%d" % hp])
                        ptoks.append("pexpb%d" % hp)
                    mid()
                    if edge:
                        A("pool", lambda e, hp=hp, ei=ei: e.tensor_tensor(out=pTs[hp][:, 0:512], in0=pexp[hp][:, 0:512],
                                                                           in1=eed[ei][:, 0:512], op=ALU.mult),
                          reads=ptoks + ["eed0"], writes=["pTs%d" % hp])
                    else:
                        A("pool" if hp == 0 else "dve", lambda e, hp=hp, h=h: e.tensor_tensor(out=pTs[hp][:, 0:640].rearrange("p (s q) -> p s q", s=5),
                                                                         in0=pexp[hp][:, 0:640].rearrange("p (s q) -> p s q", s=5),
                                                                         in1=E_int[:, :, h, :], op=ALU.mult),
                          reads=ptoks + E_INT_TOK, writes=["pTs%d" % hp])
                    pbk = 6 + h // 4
                    for i, kvj in enumerate(W):
                        A("pe", lambda e, i=i, kvj=kvj, hp=hp, h=h: e.matmul(po[h // 4][:, h % 4, :], lhsT=pTs[hp][:, i * 128:(i + 1) * 128],
                                                                               rhs=Vt[:, kvj % RK, h, :], start=(i == 0), stop=(i == nW - 1)),
                          reads=["pTs%d" % hp, "V%d" % (kvj % RK)], writes=["B%d" % pbk])

                scores(0, 0)
                scores(0, 1)
                for h in range(8):
                    if h + 1 < 8:
                        softmax_pv(h, lambda h=h: (scores(h + 1, 0), scores(h + 1, 1)))
                    else:
                        softmax_pv(h, lambda: None)
                for hh in range(2):
                    A("dve", lambda e, hh=hh: e.reciprocal(out=sm[:, 96 + hh * 4:100 + hh * 4], in_=po[hh][:, :, 64]), reads=["B%d" % (6 + hh)],
                      writes=["rec%d" % hh])
                    A("dve", lambda e, hh=hh: e.tensor_tensor(out=a_o[:, hh * 256:(hh + 1) * 256].rearrange("p (h d) -> p h d", h=4),
                                                              in0=po[hh][:, :, 0:64],
                                                              in1=sm[:, 96 + hh * 4:100 + hh * 4].unsqueeze(2).to_broadcast([128, 4, 64]), op=ALU.mult),
                      reads=["B%d" % (6 + hh), "rec%d" % hh], writes=["a_o%d" % hh])
                rms_to(a_o[:], ["a_o0", "a_o1"], mixed[:, rq, 0:512], "mixA%d" % rq, 104, junk[:])

            def step_C(m):
                rq = m % RQ
                t = "xres%d" % m

                def evac_m(pv, tk):
                    A("dve", lambda e: e.tensor_tensor(out=mT[:], in0=pv, in1=mixg[:].unsqueeze(2).to_broadcast([128, 8, 128]), op=ALU.mult),
                      reads=[tk, "mixg"], writes=["mT"])
                transpose_to(mixed[:, rq, :], ["mixA%d" % rq, "mixS%d" % rq], evac_m, 3)
                for half in range(2):
                    for k in range(8):
                        A("pe", lambda e, half=half, k=k: e.matmul(B[3][:], lhsT=mT[:, k, :], rhs=w_out[:, k, half * 512:(half + 1) * 512],
                                                                   start=(k == 0), stop=(k == 7)), reads=["mT", "w_out"], writes=["B3"])
                    A("dve", lambda e, half=half: e.scalar_tensor_tensor(out=x_res[:, m, half * 512:(half + 1) * 512],
                                                                         in0=x_res[:, m, half * 512:(half + 1) * 512], scalar=ALPHA,
                                                                         in1=B[3][:], op0=ALU.mult, op1=ALU.add),
                      reads=[t, "B3"], writes=[t])
                ln_tile(m, ln1g[:], ln1b[:], ["ln1g", "ln1b"])
                r2 = m % 2
                A("act", lambda e: e.copy(out=xbc[:], in_=x_res[:, m, :]), reads=[t], writes=["xbc"])
                transpose_to(xbc[:], ["xbc"],
                             lambda pv, tk: A("act", lambda e: e.copy(out=x1T[:, :, m * 128:(m + 1) * 128], in_=pv), reads=[tk], writes=["x1T%d" % m]), 3)
                for k in range(8):
                    A("pe", lambda e, k=k: e.matmul(B[3][:, 0:20], lhsT=x1T[:, k, m * 128:(m + 1) * 128], rhs=wr[:, k, :],
                                                    start=(k == 0), stop=(k == 7)), reads=["x1T%d" % m, "wr"], writes=["B3"])
                rs = rscr[m % 2]
                A("dve", lambda e: e.tensor_tensor(out=rs[:, 0:20], in0=B[3][:, 0:20], in1=br[:], op=ALU.add), reads=["B3", "br"],
                  writes=["rscr%d" % (m % 2)])

            def router_chain(m):
                rs = rscr[m % 2]
                tk = ["rscr%d" % (m % 2)]
                R = lambda a, b: rs[:, a:b]
                D = lambda fn: A("dve", fn, reads=tk, writes=tk)
                D(lambda e: e.tensor_reduce(out=R(20, 21), in_=R(0, 4), axis=AX.X, op=ALU.max))
                D(lambda e: e.tensor_scalar(out=R(21, 22), in0=R(20, 21), scalar1=-1.0, scalar2=None, op0=ALU.mult))
                A("act", lambda e: e.activation(out=R(24, 28), in_=R(0, 4), func=AF.Exp, bias=R(21, 22), scale=1.0, accum_out=R(22, 23)),
                  reads=tk, writes=tk)
                D(lambda e: e.reciprocal(out=R(23, 24), in_=R(22, 23)))
                D(lambda e: e.tensor_scalar(out=R(28, 32), in0=R(0, 4), scalar1=R(20, 21), scalar2=None, op0=ALU.is_equal))
                D(lambda e: e.tensor_tensor(out=rs[:, 32:48].rearrange("p (g e) -> p g e", g=4),
                                            in0=rs[:, 4:20].rearrange("p (g e) -> p g e", g=4),
                                            in1=R(28, 32).unsqueeze(2).to_broadcast([128, 4, 4]), op=ALU.mult))
                D(lambda e: e.tensor_reduce(out=R(48, 52), in_=rs[:, 32:48].rearrange("p (g e) -> p e g", g=4), axis=AX.X, op=ALU.add))
                D(lambda e: e.tensor_reduce(out=R(52, 53), in_=R(48, 52), axis=AX.X, op=ALU.max))
                D(lambda e: e.tensor_scalar(out=R(56, 60), in0=R(48, 52), scalar1=R(52, 53), scalar2=None, op0=ALU.is_equal))
                D(lambda e: e.scalar_tensor_tensor(out=R(60, 64), in0=R(56, 60), scalar=-1e30, in1=R(48, 52), op0=ALU.mult, op1=ALU.add))
                D(lambda e: e.tensor_reduce(out=R(53, 54), in_=R(60, 64), axis=AX.X, op=ALU.max))
                D(lambda e: e.tensor_scalar(out=R(64, 68), in0=R(60, 64), scalar1=R(53, 54), scalar2=None, op0=ALU.is_equal))
                D(lambda e: e.tensor_tensor(out=R(54, 55), in0=R(53, 54), in1=R(52, 53), op=ALU.subtract))
                A("act", lambda e: e.activation(out=R(55, 56), in_=R(54, 55), func=AF.Exp), reads=tk, writes=tk)
                D(lambda e: e.tensor_scalar(out=R(68, 69), in0=R(55, 56), scalar1=1.0, scalar2=None, op0=ALU.add))
                D(lambda e: e.reciprocal(out=R(69, 70), in_=R(68, 69)))
                D(lambda e: e.tensor_tensor(out=R(70, 71), in0=R(69, 70), in1=R(55, 56), op=ALU.mult))
                D(lambda e: e.tensor_scalar(out=R(72, 76), in0=R(56, 60), scalar1=R(69, 70), scalar2=None, op0=ALU.mult))
                D(lambda e: e.scalar_tensor_tensor(out=R(76, 80), in0=R(64, 68), scalar=R(70, 71), in1=R(72, 76), op0=ALU.mult, op1=ALU.add))
                D(lambda e: e.tensor_scalar(out=R(80, 84), in0=R(76, 80), scalar1=R(23, 24), scalar2=1.0 / ALPHA, op0=ALU.mult, op1=ALU.mult))
                A("dve", lambda e: e.tensor_tensor(out=comb[:, m, :].rearrange("p (g e) -> p g e", g=4),
                                                   in0=R(28, 32).unsqueeze(2).to_broadcast([128, 4, 4]),
                                                   in1=R(80, 84).unsqueeze(1).to_broadcast([128, 4, 4]), op=ALU.mult), reads=tk, writes=["comb%d" % m])

            aseq = []
            for j in range(n_kv):
                mq = j - q_off - 2
                if 0 <= mq < n_q:
                    aseq.append(("q", mq))
                aseq.append(("kv", j))
            for mq in range(n_q):
                if ("q", mq) not in aseq:
                    aseq.append(("q", mq))
            apos = [0]

            def emit_A_until(kv_need, q_need, lists=None):
                def have():
                    donev = [u for u in aseq[:apos[0]]]
                    return (("kv", kv_need) in donev or kv_need < 0) and (("q", q_need) in donev or q_need < 0)
                while not have():
                    kind, idx = aseq[apos[0]]
                    apos[0] += 1
                    if lists is None:
                        (A_kv if kind == "kv" else A_q)(idx)
                    else:
                        S.capture_begin()
                        (A_kv if kind == "kv" else A_q)(idx)
                        lists[kind].extend(S.capture_end())

            def cap(fn):
                S.capture_begin()
                fn()
                return S.capture_end()

            emit_A_until(max(win(0)), 0)
            prologue = S.capture_end()
            in_prologue[0] = False
            S.commit_interleaved([pending_ln2[0] if pending_ln2 is not None else [], prologue])
            for m in range(n_q):
                lc = cap(lambda: (step_C(m - 1) if m >= 1 else None))
                la = {"kv": [], "q": []}
                if m + 1 < n_q:
                    emit_A_until(max(win(m + 1)), m + 1, la)
                else:
                    emit_A_until(n_kv - 1, n_q - 1, la)
                ly = cap(lambda: step_B(m))
                lz = cap(lambda: (router_chain(m - 2) if m >= 2 else None))
                lg = cap(lambda: sgu_chain(m))
                S.commit_interleaved([ly, lc, la["q"], la["kv"], lz, lg], bias=[0.05, 0.0, 0.0, -0.25, 0.0, 0.0])
            def expert_dma(ex):
                ebi = ex % 2
                A("pool", lambda e: e.dma_start(out=wg_b[ebi], in_=wg_d[l, ex].rearrange("(k p) f -> p k f", p=128)),
                  writes=["wg%d" % ebi], kind="d", dsem="ewg%d" % ebi)
                A("pool", lambda e: e.dma_start(out=wu_b[ebi], in_=wu_d[l, ex].rearrange("(k p) f -> p k f", p=128)),
                  writes=["wu%d" % ebi], kind="d", dsem="ewu%d" % ebi)
                A("pool", lambda e: e.dma_start(out=wd_b[ebi], in_=wd_d[l, ex].rearrange("(c p) n -> p c n", p=128)),
                  writes=["wd%d" % ebi], kind="d", dsem="ewd%d" % ebi)

            def early_experts():
                A("pool", lambda e: e.memset(sm[:, 127:128], 0.0),
                  writes=[t for t in RING_TOK if t not in ("mixA2", "mixA3", "mixS2", "mixS3")] + EXP_TOK)
                expert_dma(0)
                expert_dma(1)

            lx = cap(lambda: step_C(n_q - 1))
            lz = cap(lambda: router_chain(n_q - 2))
            le = cap(early_experts)
            S.commit_interleaved([lx, lz, le])
            router_chain(n_q - 1)
            assert apos[0] == len(aseq)

            n_tg = n_q // 4
            yrot = [5, 6, 7, 0]
            yi = [0]

            def emit_down(ex, tg, ri, ebi):
                for t in range(4):
                    tile = tg * 4 + t
                    tt = "xres%d" % tile
                    for half in range(2):
                        yb = yrot[yi[0] % 4]
                        yi[0] += 1
                        for fc in range(2):
                            A("pe", lambda e, fc=fc, t=t, half=half, yb=yb, ri=ri, ebi=ebi: e.matmul(
                                B[yb][:], lhsT=act_b[ri][:, fc, t * 128:(t + 1) * 128], rhs=wd_b[ebi][:, fc, half * 512:(half + 1) * 512],
                                start=(fc == 0), stop=(fc == 1)), reads=["act%d_0" % ri, "act%d_1" % ri, "wd%d" % ebi],
                              writes=["B%d" % yb])
                        A("dve", lambda e, tile=tile, half=half, yb=yb, ex=ex: e.scalar_tensor_tensor(
                            out=x_res[:, tile, half * 512:(half + 1) * 512], in0=B[yb][:], scalar=comb[:, tile, ex:ex + 1],
                            in1=x_res[:, tile, half * 512:(half + 1) * 512], op0=ALU.mult, op1=ALU.add),
                          reads=["B%d" % yb, tt, "comb%d" % tile], writes=[tt])

            A("pool", lambda e: e.memset(sm[:, 126:127], 0.0), writes=["mixA2", "mixA3", "mixS2", "mixS3"] + SGACT_TOK)
            pending = None
            hcount = 0
            for ex in range(16):
                ebi = ex % 2
                if prefetch is not None and ex in (2, 5, 8, 11, 13):
                    mixer_weight_chunk(prefetch, (2, 5, 8, 11, 13).index(ex))
                if ex >= 2:
                    expert_dma(ex)
                for tg in range(n_tg):
                    ri = (ex * n_tg + tg) % 2
                    x1toks = ["x1T%d" % (tg * 4 + t) for t in range(4)]
                    for fc in range(2):
                        hs = hcount % 2
                        hcount += 1
                        gb, ub = (1, 2) if hs == 0 else (3, 4)
                        for k in range(8):
                            A("pe", lambda e, fc=fc, k=k, ebi=ebi, tg=tg, gb=gb: e.matmul(B[gb][:], lhsT=wg_b[ebi][:, k, fc * 128:(fc + 1) * 128],
                                                                                          rhs=x1T[:, k, tg * 512:(tg + 1) * 512], start=(k == 0), stop=(k == 7)),
                              reads=x1toks + ["wg%d" % ebi], writes=["B%d" % gb])
                        for k in range(8):
                            A("pe", lambda e, fc=fc, k=k, ebi=ebi, tg=tg, ub=ub: e.matmul(B[ub][:], lhsT=wu_b[ebi][:, k, fc * 128:(fc + 1) * 128],
                                                                                          rhs=x1T[:, k, tg * 512:(tg + 1) * 512], start=(k == 0), stop=(k == 7)),
                              reads=x1toks + ["wu%d" % ebi], writes=["B%d" % ub])
                        A("act", lambda e, hs=hs, gb=gb: e.activation(out=sg_b[hs][:, 0, :], in_=B[gb][:], func=AF.Silu),
                          reads=["B%d" % gb], writes=["sg%d_0" % hs])
                        A("dve", lambda e, fc=fc, ri=ri, hs=hs, ub=ub: e.tensor_tensor(out=act_b[ri][:, fc, :], in0=sg_b[hs][:, 0, :], in1=B[ub][:], op=ALU.mult),
                          reads=["sg%d_0" % hs, "B%d" % ub], writes=["act%d_%d" % (ri, fc)])
                        if fc == 0 and pending is not None:
                            emit_down(*pending)
                            pending = None
                    pending = (ex, tg, ri, ebi)
            emit_down(*pending)

            S.capture_begin()
            A("sp", lambda e: e.dma_start(out=ln1g[:], in_=ln2g_d[l].partition_broadcast(128)), writes=["ln1g"], kind="d", dsem="c11")
            A("sp", lambda e: e.dma_start(out=ln1b[:], in_=ln2b_d[l].partition_broadcast(128)), writes=["ln1b"], kind="d", dsem="c12")
            ends = {}
            for m in range(n_q):
                ln_tile(m, ln1g[:], ln1b[:], ["ln1g", "ln1b"], eps=EPS / (ALPHA * ALPHA))
                gt = sbk * 8 + m
                dap = dst_tile_ap(gt)
                A("sp", lambda e, m=m, dap=dap: e.dma_start(out=dap, in_=x_res[:, m, :]), reads=["xres%d" % m],
                  writes=[dst_tok(gt)], kind="d", dsem="xw%d" % gt)
                if (not last_layer) and gt >= 14 and use_cc:
                    ci = cc_in.ap()
                    A("sp", lambda e, m=m, gt=gt, ci=ci: e.dma_start(out=ci[(gt - 14) * 128:(gt - 13) * 128, :], in_=x_res[:, m, :]),
                      reads=["xres%d" % m], writes=["ccin%d" % (gt - 14)], kind="d", dsem="xw%d" % (gt + 2))
                ends[m] = len(S._cap)
            ln_next = l if sbk == 0 else (l + 1 if not last_layer else None)
            if ln_next is not None:
                A("sp", lambda e: e.dma_start(out=ln1g[:], in_=ln1g_d[ln_next].partition_broadcast(128)), writes=["ln1g"], kind="d", dsem="c8")
                A("sp", lambda e: e.dma_start(out=ln1b[:], in_=ln1b_d[ln_next].partition_broadcast(128)), writes=["ln1b"], kind="d", dsem="c9")
            if (not last_layer) and sbk == 1 and use_cc:
                A("pool", lambda e: e.collective_compute("AllGather", ALU.bypass, replica_groups=[[0, 1], [2, 3], [4, 5], [6, 7]],
                                                         ins=[cc_in.ap().opt()], outs=[cc_out.ap().opt()]),
                  reads=["ccin0", "ccin1"], writes=["gath"], kind="cc", dsem="cc%d" % l)
            return (S.capture_end(), ends)

        pend = [None]
        for l in range(depth):
            last = (l == depth - 1)
            if l == 0:
                src = lambda t: (x_d[t * 128:(t + 1) * 128, :], "x_in")
            else:
                sbuf_ = bufs[(l - 1) % 2]
                src = (lambda sb_, bi: (lambda t: (sb_[t * 128:(t + 1) * 128, :], "xd%d_%d" % (bi, t))))(sbuf_, (l - 1) % 2)
            if last:
                dst = lambda t: out_d[t * 128:(t + 1) * 128, :]
                dtok = lambda t: "outd_%d" % t
            else:
                dbuf_ = bufs[l % 2]
                dst = (lambda db_: (lambda t: db_[t * 128:(t + 1) * 128, :]))(dbuf_)
                dtok = (lambda bi: (lambda t: "xd%d_%d" % (bi, t)))(l % 2)
            layer_consts(l)
            if l == 0:
                for q4 in range(5):
                    mixer_weight_chunk(0, q4)
            for sbk in range(2):
                pend[0] = subblock(l, sbk, src, dst, dtok, last, (l + 1) if (sbk == 1 and not last) else None, pend[0])
        S.commit_interleaved([pend[0][0]])
        final = ["xw%d" % t for t in range(16)]
        S.emit(sems, dsems, final_wait=final)
    return nc


GRID_W = 64
SEQ = 4096


def _token_map(flip):
    u = np.arange((N_OWN + 2) * 128)
    return (SEQ - 1 - u) if flip else u


def _bias_tables(rel_bias, flip):
    gmap = _token_map(flip)
    H = rel_bias.shape[0]

    def slot(m, kv):
        gq = gmap[m * 128:(m + 1) * 128]
        gk = gmap[kv * 128:(kv + 1) * 128]
        rq, cq = gq // GRID_W, gq % GRID_W
        rk, ck = gk // GRID_W, gk % GRID_W
        rs = np.clip(rq - 4, 0, 56)
        cs = np.clip(cq - 8, 0, 48)
        ok = (rk[:, None] >= rs[None, :]) & (rk[:, None] < rs[None, :] + 8)
        ok &= (ck[:, None] >= cs[None, :]) & (ck[:, None] < cs[None, :] + 16)
        dr = np.clip(rk[:, None] - rq[None, :] + 7, 0, 14)
        dc = np.clip(ck[:, None] - cq[None, :] + 15, 0, 30)
        tab = rel_bias[:, dr, dc]
        return np.where(ok[None], tab, np.float32(NEG)).astype(np.float32)

    interior = np.empty((128, 5, H, 128), np.float32)
    for i in range(5):
        interior[:, i] = slot(4, 2 + i).transpose(1, 0, 2)
    edge = np.empty((2, H, 128, 4, 128), np.float32)
    for m in range(2):
        for i in range(4):
            edge[m, :, :, i, :] = slot(m, i)
    return interior.reshape(128, 5 * H * 128), edge.reshape(2 * H, 128, 512)


_PROG = {}


def _get_prog(depth):
    if depth not in _PROG:
        _PROG[depth] = build_program(depth)
    return _PROG[depth]


def kernel(x, w_in, w_out, na_rel_bias, sgu_ln_g, sgu_ln_b, sgu_w, sgu_b, mix_norm_g,
           ln1_g, ln1_b, router_group_w, router_group_b, router_expert_w, router_expert_b,
           expert_w_gate, expert_w_up, expert_w_down, ln2_g, ln2_b):
    f = lambda a: np.ascontiguousarray(np.asarray(a, dtype=np.float32))
    x = f(x)
    Bsz, T, D = x.shape
    depth = w_in.shape[0]
    nc = _get_prog(depth)
    ident = np.eye(128, dtype=np.float32)
    jm = np.ascontiguousarray(ident[::-1])
    shared = dict(
        w_in=f(w_in), w_out=f(w_out),
        sgu_g=f(sgu_ln_g).reshape(depth, 1, 512), sgu_b=f(sgu_ln_b).reshape(depth, 1, 512),
        mixg=f(np.asarray(mix_norm_g).reshape(depth, 8, 128).transpose(0, 2, 1)),
        ln1g=f(ln1_g).reshape(depth, 1, 1024), ln1b=f(ln1_b).reshape(depth, 1, 1024),
        ln2g=f(ln2_g).reshape(depth, 1, 1024), ln2b=f(ln2_b).reshape(depth, 1, 1024),
        wr=f(np.concatenate([np.asarray(router_group_w), np.asarray(router_expert_w)], axis=2)),
        br=f(np.concatenate([np.asarray(router_group_b), np.asarray(router_expert_b)], axis=1)).reshape(depth, 1, 20),
        wg=f(expert_w_gate), wu=f(expert_w_up), wd=f(expert_w_down), ident=ident, jmat=jm)
    sw = np.asarray(sgu_w, dtype=np.float32)
    sbias = np.asarray(sgu_b, dtype=np.float32)
    rb = np.asarray(na_rel_bias, dtype=np.float32)
    per_type = []
    for flip in (False, True):
        swl = sw[:, :, ::-1, ::-1] if flip else sw
        sbl = sbias[:, :, ::-1] if flip else sbias
        tabs = [_bias_tables(rb[l], flip) for l in range(depth)]
        selv = np.zeros((128, 2), np.float32)
        selv[:, 0 if flip else 1] = 1.0
        per_type.append(dict(wsT=f(swl.transpose(0, 3, 1, 2).reshape(depth, 128, 1024)), bs=f(sbl.transpose(0, 2, 1)),
                             bm_int=f(np.stack([t[0] for t in tabs])), bm_edge=f(np.stack([t[1] for t in tabs])), sel=selv))
    in_maps = []
    for c in range(8):
        b, flip = c // 2, c % 2
        gmap = _token_map(bool(flip))
        d = dict(shared)
        d.update(per_type[flip])
        d["x"] = f(x[b, gmap])
        in_maps.append(d)
    res = run_bass_kernel_spmd(nc, in_maps, core_ids=list(range(8)))
    out = np.empty_like(x)
    for c in range(8):
        b, flip = c // 2, c % 2
        gmap = _token_map(bool(flip))
        out[b, gmap[:N_OWN * 128]] = res.results[c]["out"]
    return out
```

```python
import contextlib
import numpy as np
import concourse.bass as bass
import concourse.mybir as mybir
from concourse.bass_utils import run_bass_kernel_spmd

F32 = mybir.dt.float32
BF16 = mybir.dt.bfloat16
AF = mybir.ActivationFunctionType
ALU = mybir.AluOpType
AX = mybir.AxisListType

ALPHA = float(8 ** 0.25)
EPS = 1e-5
NEG = -30000.0
ENGS = ("pe", "act", "dve", "pool", "sp")


class Sched:
    def __init__(self, nc):
        self.nc = nc
        self.ops = []
        self.last_writer = {}
        self.readers = {}

    def capture_begin(self):
        self._cap = []

    def capture_end(self):
        c, self._cap = self._cap, None
        return c

    def commit_interleaved(self, lists, bias=None):
        has0 = bool(lists) and bool(lists[0])
        if bias is None:
            bias = [0.0] * len(lists)
        bias = [b for l, b in zip(lists, bias) if l]
        lists = [l for l in lists if l]
        pos = [0] * len(lists)
        while True:
            best, bf = None, None
            for i, l in enumerate(lists):
                if pos[i] < len(l):
                    f = pos[i] / len(l) - bias[i]
                    if bf is None or f < bf:
                        best, bf = i, f
            if best is None:
                break
            item = lists[best][pos[best]]
            if has0 and item[6] is not None and best != 0 and pos[0] < min(item[6], len(lists[0])):
                best = 0
                item = lists[0][pos[0]]
            self.add(*item[:6])
            pos[best] += 1
            while len(item) > 7 and item[7] and pos[best] < len(lists[best]):
                item = lists[best][pos[best]]
                self.add(*item[:6])
                pos[best] += 1

    def add(self, eng, fn, reads=(), writes=(), kind="c", dsem=None, req=None, glue=False):
        if getattr(self, "_cap", None) is not None:
            self._cap.append((eng, fn, tuple(reads), tuple(writes), kind, dsem, req, glue))
            return -1
        idx = len(self.ops)
        deps = {}
        for r in reads:
            w = self.last_writer.get(r)
            if w is not None:
                deps[w] = "raw"
        for wr in writes:
            w = self.last_writer.get(wr)
            if w is not None and w not in deps:
                deps[w] = "waw"
            rd = self.readers.get(wr)
            if rd:
                for j in rd[0].values():
                    if j not in deps:
                        deps[j] = "war"
                for j in rd[1]:
                    if j not in deps:
                        deps[j] = "war"
        for r in reads:
            rd = self.readers.setdefault(r, ({}, []))
            if kind != "c":
                rd[1].append(idx)
            else:
                rd[0][eng] = idx
        for wr in writes:
            self.last_writer[wr] = idx
            self.readers[wr] = ({}, [])
        self.ops.append(dict(eng=eng, fn=fn, deps=deps, kind=kind, dsem=dsem,
                             signal=False, sigval=None))
        return idx

    def emit(self, sems, dsems, final_wait=()):
        ops = self.ops
        need = [[] for _ in ops]
        for i, op in enumerate(ops):
            for j, typ in op["deps"].items():
                src = ops[j]
                if src["kind"] == "c" and op["kind"] == "c" and src["eng"] == op["eng"]:
                    if op["eng"] == "pe" or typ != "raw":
                        continue
                need[i].append(j)
                src["signal"] = True
        cnt = {e: 0 for e in ENGS}
        dcnt = {k: 0 for k in dsems}
        for op in ops:
            if op["kind"] in ("d", "cc"):
                dcnt[op["dsem"]] += 16 if op["kind"] == "d" else 1
                op["sigval"] = (op["dsem"], dcnt[op["dsem"]])
            elif op["signal"]:
                cnt[op["eng"]] += 1
                op["sigval"] = (op["eng"], cnt[op["eng"]])

        def run(engname, eng):
            waited = {}
            for i, op in enumerate(ops):
                if op["eng"] != engname:
                    continue
                req = {}
                for j in need[i]:
                    k, v = ops[j]["sigval"]
                    if v > req.get(k, 0):
                        req[k] = v
                for k, v in req.items():
                    if waited.get(k, 0) >= v:
                        continue
                    waited[k] = v
                    eng.wait_ge(sems[k] if k in sems else dsems[k], v)
                ins = op["fn"](eng)
                if op["kind"] == "d":
                    ins.then_inc(dsems[op["dsem"]], 16)
                elif op["kind"] == "cc":
                    ins.then_inc(dsems[op["dsem"]], 1)
                elif op["signal"]:
                    ins.then_inc(sems[op["eng"]], 1)
            if engname == "sp":
                for k in final_wait:
                    eng.wait_ge(dsems[k], dcnt[k])

        with self.nc.Block() as block:
            @block.tensor
            def _(e):
                run("pe", e)

            @block.scalar
            def _(e):
                run("act", e)

            @block.vector
            def _(e):
                run("dve", e)

            @block.gpsimd
            def _(e):
                run("pool", e)

            @block.sync
            def _(e):
                run("sp", e)


N_OWN = 16
N_Q = 8


def build_program(depth=4, use_cc=True):
    n_q = N_Q
    NT = n_q * 128
    RK = 8
    RQ = 4
    nc = bass.Bass("TRN2", target_bir_lowering=False)

    def din(name, shape):
        return nc.dram_tensor(name, shape, F32, kind="ExternalInput").ap()

    L = depth
    x_d = din("x", [(N_OWN + 2) * 128, 1024])
    w_in_d = din("w_in", [L, 1024, 2560])
    w_out_d = din("w_out", [L, 1024, 1024])
    bm_int_d = din("bm_int", [L, 128, 5 * 1024])
    bm_edge_d = din("bm_edge", [L, 16, 128, 512])
    wsT_d = din("wsT", [L, 128, 1024])
    bs_d = din("bs", [L, 128, 8])
    sgug_d = din("sgu_g", [L, 1, 512])
    sgub_d = din("sgu_b", [L, 1, 512])
    mixg_d = din("mixg", [L, 128, 8])
    ln1g_d = din("ln1g", [L, 1, 1024])
    ln1b_d = din("ln1b", [L, 1, 1024])
    ln2g_d = din("ln2g", [L, 1, 1024])
    ln2b_d = din("ln2b", [L, 1, 1024])
    wr_d = din("wr", [L, 1024, 20])
    br_d = din("br", [L, 1, 20])
    wg_d = din("wg", [L, 16, 1024, 256])
    wu_d = din("wu", [L, 16, 1024, 256])
    wd_d = din("wd", [L, 16, 256, 1024])
    ident_d = din("ident", [128, 128])
    jmat_d = din("jmat", [128, 128])
    sel_d = din("sel", [128, 2])
    out_d = nc.dram_tensor("out", [N_OWN * 128, 1024], F32, kind="ExternalOutput").ap()
    bufs = [nc.dram_tensor("xbuf%d" % i, [N_OWN * 128, 1024], F32).ap() for i in range(2)]
    cc_in = nc.dram_tensor("cc_in", [256, 1024], F32)
    cc_out = nc.dram_tensor("cc_out", [512, 1024], F32)

    with contextlib.ExitStack() as es:
        def sb(name, shape, dt=F32):
            return es.enter_context(nc.sbuf_tensor("sb_" + name, shape, dt))

        def ps(name, shape, dt=F32):
            return es.enter_context(nc.psum_tensor("ps_" + name, shape, dt))

        x_res = sb("x_res", [128, n_q, 1024])
        w_in = sb("w_in", [128, 8, 2560], BF16)
        w_out = sb("w_out", [128, 8, 1024], BF16)
        E_int = sb("E_int", [128, 8, 5, 128], BF16)
        est = [sb("est0", [128, 512])] * 2
        eed = [sb("eed0", [128, 512], BF16)] * 2
        xst = [sb("xst%d" % i, [128, 1024]) for i in range(2)]
        xb = [sb("xb%d" % i, [128, 1024], BF16) for i in range(2)]
        xT = [sb("xT%d" % i, [128, 8, 128], BF16) for i in range(3)]
        arena = sb("arena", [128, 15360], BF16)
        kT = arena[:, 0:4096].rearrange("p (r c n) -> p r c n", r=RK, c=4)
        Vt = arena[:, 4096:8256].rearrange("p (r h d) -> p r h d", r=RK, h=8)
        qT = arena[:, 8256:10304].rearrange("p (r c n) -> p r c n", r=RQ, c=4)
        gu_ = [sb("gu%d" % i, [128, 512]) for i in range(2)]
        gv_ = [sb("gv%d" % i, [128, 512]) for i in range(2)]
        sq = sb("sq", [128, 512])
        vn = sb("vn", [128, 512], BF16)
        mixed = arena[:, 10304:14400].rearrange("p (r n) -> p r n", r=RQ)
        pexp = [sb("pexp%d" % i, [128, 640], BF16) for i in range(2)]
        pTs = [sb("pTs%d" % i, [128, 640], BF16) for i in range(2)]
        a_o = sb("a_o", [128, 512])
        mT = sb("mT", [128, 8, 128], BF16)
        kb_ = sb("kb_", [128, 512], BF16)
        qb_ = sb("qb_", [128, 512], BF16)
        x1T = sb("x1T", [128, 8, NT], BF16)
        ln1g = sb("ln1g", [128, 1024])
        ln1b = sb("ln1b", [128, 1024])
        sgug = sb("sgug", [128, 512])
        sgub = sb("sgub", [128, 512])
        wsT = sb("wsT", [128, 8, 128], BF16)
        wr = sb("wr", [128, 8, 20], BF16)
        br = sb("br", [128, 20])
        bs = sb("bs", [128, 8])
        mixg = sb("mixg", [128, 8])
        ident = sb("ident", [128, 128], BF16)
        jmat = sb("jmat", [128, 128], BF16)
        sel = sb("sel", [128, 2])
        nh = sb("nh", [128, 8])
        comb = sb("comb", [128, n_q, 16])
        sm = sb("sm", [128, 192])
        junk = sb("junk", [128, 512], BF16)
        xbc = sb("xbc", [128, 1024], BF16)
        rscr = [sb("rscr%d" % i, [128, 96]) for i in range(2)]
        wg_b, wu_b, wd_b = [], [], []
        for i in range(2):
            o = i * 6144
            wg_b.append(arena[:, o:o + 2048].rearrange("p (k f) -> p k f", k=8))
            wu_b.append(arena[:, o + 2048:o + 4096].rearrange("p (k f) -> p k f", k=8))
            wd_b.append(arena[:, o + 4096:o + 6144].rearrange("p (c n) -> p c n", c=2))
        sg_b = [arena[:, 12288 + i * 512:12288 + (i + 1) * 512].rearrange("p (c n) -> p c n", c=1) for i in range(2)]
        act_b = [arena[:, 13312 + i * 1024:13312 + (i + 1) * 1024].rearrange("p (c n) -> p c n", c=2) for i in range(2)]

        B = [ps("B%d" % i, [128, 512]) for i in range(8)]
        pTv = [B[i][:].bitcast(BF16).rearrange("p (k n) -> p k n", k=8) for i in range(8)]

        sems = {e: es.enter_context(nc.semaphore("s_" + e)) for e in ENGS}
        dnames = (["c%d" % i for i in range(16)] + ["xl%d" % i for i in range(12)] + ["xw%d" % i for i in range(18)]
                  + ["ee0", "ee1", "ewg0", "ewg1", "ewu0", "ewu1", "ewd0", "ewd1", "st", "win0", "win1", "win2", "win3", "ei0", "ei1", "cc0", "cc1", "cc2", "cc3", "ga", "gb"])
        dsems = {k: es.enter_context(nc.semaphore("d_" + k)) for k in dnames}
        S = Sched(nc)
        A = S.add

        W_IN_TOK = ["w_in_l%d" % i for i in range(4)]
        E_INT_TOK = ["E_int%d" % s for s in range(5)]
        EXP_TOK = ["wg0", "wu0", "wd0", "wg1", "wu1", "wd1"]
        RING_TOK = (["kT%d" % i for i in range(RK)] + ["V%d" % i for i in range(RK)] + ["qT%d" % i for i in range(RQ)]
                    + ["mixA%d" % i for i in range(RQ)] + ["mixS%d" % i for i in range(RQ)] + ["Vones"])
        SGACT_TOK = ["sg%d_%d" % (i, f) for i in range(2) for f in range(2)] + ["act%d_%d" % (i, f) for i in range(2) for f in range(2)]

        A("pool", lambda e: e.memset(nh[:], -0.5), writes=["nh"])
        ident_f = xst[0][:, 0:128]
        A("sp", lambda e: e.dma_start(out=ident_f, in_=ident_d), writes=["xst0"], kind="d", dsem="c0")
        A("dve", lambda e: e.tensor_copy(out=ident[:], in_=ident_f), reads=["xst0"], writes=["ident"])
        A("sp", lambda e: e.dma_start(out=ident_f, in_=jmat_d), reads=["ident"], writes=["xst0"], kind="d", dsem="c13")
        A("dve", lambda e: e.tensor_copy(out=jmat[:], in_=ident_f), reads=["xst0"], writes=["jmat"])
        A("sp", lambda e: e.dma_start(out=sel[:], in_=sel_d), writes=["sel"], kind="d", dsem="c14")

        def ln_tile(m, g_ap, b_ap, gtoks, eps=EPS):
            xr = x_res[:, m, :]
            t = "xres%d" % m
            o = 128 * (m % 2)
            sfx = "_%d" % (m % 2)
            A("dve", lambda e: e.bn_stats(out=sm[:, o:o + 6], in_=xr[:, 0:512]), reads=[t], writes=["st0" + sfx])
            A("dve", lambda e: e.bn_stats(out=sm[:, o + 6:o + 12], in_=xr[:, 512:1024]), reads=[t], writes=["st1" + sfx])
            A("dve", lambda e: e.bn_aggr(out=sm[:, o + 12:o + 14], in_=sm[:, o:o + 12]), reads=["st0" + sfx, "st1" + sfx], writes=["mv" + sfx])
            A("dve", lambda e: e.tensor_scalar(out=sm[:, o + 14:o + 15], in0=sm[:, o + 13:o + 14], scalar1=eps, scalar2=None, op0=ALU.add),
              reads=["mv" + sfx], writes=["ve" + sfx])
            A("pool", lambda e: e.tensor_tensor(out=sm[:, o + 15:o + 16], in0=sm[:, o + 14:o + 15], in1=nh[:, 0:1], op=ALU.pow),
              reads=["ve" + sfx, "nh"], writes=["rstd" + sfx])
            A("dve", lambda e: e.scalar_tensor_tensor(out=xr, in0=xr, scalar=sm[:, o + 12:o + 13], in1=g_ap, op0=ALU.subtract, op1=ALU.mult),
              reads=[t, "mv" + sfx, gtoks[0]], writes=[t])
            A("dve", lambda e: e.scalar_tensor_tensor(out=xr, in0=xr, scalar=sm[:, o + 15:o + 16], in1=b_ap, op0=ALU.mult, op1=ALU.add),
              reads=[t, "rstd" + sfx, gtoks[1]], writes=[t])

        def rms_to(src_ap, src_tok, dst_ap, dst_tok, col, jk):
            A("act", lambda e: e.activation(out=jk, in_=src_ap, func=AF.Square, accum_out=sm[:, col:col + 1]),
              reads=list(src_tok), writes=["ss%d" % col])
            A("dve", lambda e: e.tensor_scalar(out=sm[:, col + 1:col + 2], in0=sm[:, col:col + 1], scalar1=1.0 / 512, scalar2=EPS,
                                               op0=ALU.mult, op1=ALU.add), reads=["ss%d" % col], writes=["sv%d" % col])
            A("pool", lambda e: e.tensor_tensor(out=sm[:, col + 2:col + 3], in0=sm[:, col + 1:col + 2], in1=nh[:, 0:1], op=ALU.pow),
              reads=["sv%d" % col, "nh"], writes=["sr%d" % col])
            A("dve", lambda e: e.tensor_scalar(out=dst_ap, in0=src_ap, scalar1=sm[:, col + 2:col + 3], scalar2=None, op0=ALU.mult),
              reads=list(src_tok) + ["sr%d" % col], writes=[dst_tok])

        def transpose_to(src_ap, src_tok, evac, bank):
            for k in range(8):
                A("pe", lambda e, k=k: e.transpose(out=pTv[bank][:, k, :], in_=src_ap[:, k * 128:(k + 1) * 128], identity=ident[:]),
                  reads=list(src_tok) + ["ident"], writes=["B%d" % bank], glue=True)
            evac(pTv[bank], "B%d" % bank)

        def layer_consts(l):
            A("pool", lambda e: e.dma_start(out=wsT[:], in_=wsT_d[l].rearrange("p (g q) -> p g q", g=8)), writes=["wsT"], kind="d", dsem="c1")
            A("pool", lambda e: e.dma_start(out=wr[:], in_=wr_d[l].rearrange("(k p) n -> p k n", p=128)), writes=["wr"], kind="d", dsem="c2")
            A("sp", lambda e: e.dma_start(out=bs[:], in_=bs_d[l]), writes=["bs"], kind="d", dsem="c4")
            A("sp", lambda e: e.dma_start(out=mixg[:], in_=mixg_d[l]), writes=["mixg"], kind="d", dsem="c5")
            A("sp", lambda e: e.dma_start(out=sgug[:], in_=sgug_d[l].partition_broadcast(128)), writes=["sgug"], kind="d", dsem="c6")
            A("sp", lambda e: e.dma_start(out=sgub[:], in_=sgub_d[l].partition_broadcast(128)), writes=["sgub"], kind="d", dsem="c7")
            if l == 0:
                A("sp", lambda e: e.dma_start(out=ln1g[:], in_=ln1g_d[l].partition_broadcast(128)), writes=["ln1g"], kind="d", dsem="c8")
                A("sp", lambda e: e.dma_start(out=ln1b[:], in_=ln1b_d[l].partition_broadcast(128)), writes=["ln1b"], kind="d", dsem="c9")
            A("sp", lambda e: e.dma_start(out=br[:], in_=br_d[l].partition_broadcast(128)), writes=["br"], kind="d", dsem="c10")
            for s in range(5):
                A("sp", lambda e, s=s: e.dma_start(out=xst[s % 2][:], in_=bm_int_d[l, :, s * 1024:(s + 1) * 1024]),
                  writes=["xst%d" % (s % 2)], kind="d", dsem="ei%d" % (s % 2))
                A("act", lambda e, s=s: e.activation(out=E_int[:, :, s, :], in_=xst[s % 2][:].rearrange("p (h q) -> p h q", h=8), func=AF.Exp),
                  reads=["xst%d" % (s % 2)], writes=["E_int%d" % s])

        def mixer_weight_chunk(l, q4):
            if q4 < 4:
                A("pool", lambda e: e.dma_start(out=w_in[:, 2 * q4:2 * q4 + 2, :],
                                                in_=w_in_d[l, q4 * 256:(q4 + 1) * 256, :].rearrange("(k p) n -> p k n", p=128)),
                  writes=["w_in_l%d" % q4], kind="d", dsem="win%d" % q4)
            else:
                A("pool", lambda e: e.dma_start(out=w_out[:], in_=w_out_d[l].rearrange("(k p) n -> p k n", p=128)),
                  writes=["w_out"], kind="d", dsem="c3")

        def arena_barrier():
            A("pool", lambda e: e.memset(sm[:, 127:128], 0.0), writes=RING_TOK + EXP_TOK + SGACT_TOK)

        def subblock(l, sbk, src_tile_ap, dst_tile_ap, dst_tok, last_layer, prefetch, pending_ln2):
            if sbk == 0:
                kv_tiles = list(range(0, 10))
                q_off = 0
            else:
                kv_tiles = list(range(6, 18))
                q_off = 2
            n_kv = len(kv_tiles)

            def win(m):
                if sbk == 0:
                    return [0, 1, 2, 3] if m < 2 else list(range(m - 2, m + 3))
                return list(range(m, m + 5))

            in_prologue = [True]
            S.capture_begin()
            arena_barrier()
            A("pool", lambda e: e.memset(Vt[:, :, :, 64:65], 1.0), writes=["Vones"])

            def A_kv(j):
                tile = kv_tiles[j]
                m = j - q_off
                own = 0 <= m < n_q
                r2 = j % 2
                r3 = j % 3
                rk = j % RK
                rev = (l > 0 and tile >= N_OWN)
                if own:
                    xsrc, xtok = x_res[:, m, :], "xres%d" % m
                else:
                    xsrc, xtok = xst[r2][:], "xst%d" % r2
                if not rev:
                    sap, stok = src_tile_ap(tile)
                    rq_ = (pending_ln2[1][m] if (own and in_prologue[0] and pending_ln2 is not None) else None)
                    A("sp", lambda e: e.dma_start(out=xsrc, in_=sap), reads=[stok], writes=[xtok], kind="d", dsem="xl%d" % j, req=rq_)
                    A("act", lambda e: e.copy(out=xb[r2][:], in_=xsrc), reads=[xtok], writes=["xb%d" % r2])
                    transpose_to(xb[r2][:], ["xb%d" % r2],
                                 lambda pv, tk: A("act", lambda e: e.copy(out=xT[r3][:], in_=pv), reads=[tk], writes=["xT%d" % r3]), 1)
                else:
                    o = (17 - tile) * 128
                    cco = cc_out.ap()
                    A("sp", lambda e: e.dma_start(out=xst[0][:], in_=cco[o:o + 128, :]), reads=["gath"], writes=["xst0"], kind="d", dsem="ga")
                    A("sp", lambda e: e.dma_start(out=xst[1][:], in_=cco[256 + o:256 + o + 128, :]), reads=["gath"], writes=["xst1"], kind="d", dsem="gb")
                    A("dve", lambda e: e.tensor_scalar(out=xst[0][:], in0=xst[0][:], scalar1=sel[:, 0:1], scalar2=None, op0=ALU.mult),
                      reads=["xst0", "sel"], writes=["xst0"])
                    A("dve", lambda e: e.scalar_tensor_tensor(out=xst[0][:], in0=xst[1][:], scalar=sel[:, 1:2], in1=xst[0][:],
                                                              op0=ALU.mult, op1=ALU.add), reads=["xst0", "xst1", "sel"], writes=["xst0"])
                    A("act", lambda e: e.copy(out=xb[r2][:], in_=xst[0][:]), reads=["xst0"], writes=["xb%d" % r2])
                    for hf in range(2):
                        for k4 in range(4):
                            k = hf * 4 + k4
                            A("pe", lambda e, k=k, k4=k4: e.matmul(B[1][:, k4 * 128:(k4 + 1) * 128], lhsT=xb[r2][:, k * 128:(k + 1) * 128],
                                                                   rhs=jmat[:], start=True, stop=True), reads=["xb%d" % r2, "jmat"], writes=["B1"])
                        A("act", lambda e, hf=hf: e.copy(out=xT[r3][:, hf * 4:(hf + 1) * 4, :], in_=B[1][:].rearrange("p (k t) -> p k t", k=4)),
                          reads=["B1"], writes=["xT%d_%d" % (r3, hf)])
                xt = xT[r3]
                xtt = ["xT%d" % r3, "xT%d_0" % r3, "xT%d_1" % r3]
                for k in range(8):
                    A("pe", lambda e, k=k: e.matmul(B[1][:], lhsT=xt[:, k, :], rhs=w_in[:, k, 512:1024], start=(k == 0), stop=(k == 7)),
                      reads=xtt + W_IN_TOK, writes=["B1"])
                A("act", lambda e: e.copy(out=kb_[:], in_=B[1][:]), reads=["B1"], writes=["kb_"])
                for c in range(4):
                    A("pe", lambda e, c=c: e.transpose(out=pTv[1][:, c, :], in_=kb_[:, c * 128:(c + 1) * 128], identity=ident[:]),
                      reads=["kb_", "ident"], writes=["B1"])
                A("act", lambda e: e.copy(out=kT[:, rk, :, :], in_=pTv[1][:, 0:4, :]), reads=["B1"], writes=["kT%d" % rk])
                for k in range(8):
                    A("pe", lambda e, k=k: e.matmul(B[1][:], lhsT=xt[:, k, :], rhs=w_in[:, k, 1024:1536], start=(k == 0), stop=(k == 7)),
                      reads=xtt + W_IN_TOK, writes=["B1"])
                A("act", lambda e: e.copy(out=Vt[:, rk, :, 0:64], in_=B[1][:].rearrange("p (h d) -> p h d", h=8)),
                  reads=["B1", "Vones"], writes=["V%d" % rk])

            def A_q(m):
                j = m + q_off
                r3 = j % 3
                xt = xT[r3]
                xtt = ["xT%d" % r3, "xT%d_0" % r3, "xT%d_1" % r3]
                rq = m % RQ
                gu, gv = gu_[m % 2], gv_[m % 2]
                for k in range(8):
                    A("pe", lambda e, k=k: e.matmul(B[2][:], lhsT=xt[:, k, :], rhs=w_in[:, k, 0:512], start=(k == 0), stop=(k == 7)),
                      reads=xtt + W_IN_TOK, writes=["B2"])
                A("act", lambda e: e.activation(out=qb_[:], in_=B[2][:], func=AF.Copy, scale=0.125), reads=["B2"], writes=["qb_"])
                for c in range(4):
                    A("pe", lambda e, c=c: e.transpose(out=pTv[2][:, c, :], in_=qb_[:, c * 128:(c + 1) * 128], identity=ident[:]),
                      reads=["qb_", "ident"], writes=["B2"])
                A("act", lambda e: e.copy(out=qT[:, rq, :, :], in_=pTv[2][:, 0:4, :]), reads=["B2"], writes=["qT%d" % rq])
                for k in range(8):
                    A("pe", lambda e, k=k: e.matmul(B[2][:], lhsT=xt[:, k, :], rhs=w_in[:, k, 1536:2048], start=(k == 0), stop=(k == 7)),
                      reads=xtt + W_IN_TOK, writes=["B2"])
                A("act", lambda e: e.activation(out=gu[:], in_=B[2][:], func=AF.Gelu_apprx_tanh), reads=["B2"], writes=["gu%d" % (m % 2)])
                for k in range(8):
                    A("pe", lambda e, k=k: e.matmul(B[2][:], lhsT=xt[:, k, :], rhs=w_in[:, k, 2048:2560], start=(k == 0), stop=(k == 7)),
                      reads=xtt + W_IN_TOK, writes=["B2"])
                A("act", lambda e: e.activation(out=gv[:], in_=B[2][:], func=AF.Gelu_apprx_tanh), reads=["B2"], writes=["gv%d" % (m % 2)])

            def sgu_chain(m):
                rq = m % RQ
                gu, gv = gu_[m % 2], gv_[m % 2]
                tgv, tgu = "gv%d" % (m % 2), "gu%d" % (m % 2)
                gv3 = gv[:].rearrange("p (g d) -> p g d", g=8)
                sq3 = sq[:].rearrange("p (g d) -> p g d", g=8)
                A("dve", lambda e: e.tensor_reduce(out=sm[:, 32:40], in_=gv3, axis=AX.X, op=ALU.add), reads=[tgv], writes=["s1"])
                A("dve", lambda e: e.tensor_tensor(out=sq[:], in0=gv[:], in1=gv[:], op=ALU.mult), reads=[tgv], writes=["sq"])
                A("dve", lambda e: e.tensor_reduce(out=sm[:, 40:48], in_=sq3, axis=AX.X, op=ALU.add), reads=["sq"], writes=["s2"])
                A("dve", lambda e: e.tensor_scalar(out=sm[:, 48:56], in0=sm[:, 32:40], scalar1=1.0 / 64, scalar2=None, op0=ALU.mult),
                  reads=["s1"], writes=["gmean"])
                A("dve", lambda e: e.tensor_tensor(out=sm[:, 56:64], in0=sm[:, 48:56], in1=sm[:, 48:56], op=ALU.mult),
                  reads=["gmean"], writes=["gmsq"])
                A("dve", lambda e: e.scalar_tensor_tensor(out=sm[:, 64:72], in0=sm[:, 40:48], scalar=1.0 / 64, in1=sm[:, 56:64],
                                                          op0=ALU.mult, op1=ALU.subtract), reads=["s2", "gmsq"], writes=["gvar"])
                A("dve", lambda e: e.tensor_scalar(out=sm[:, 72:80], in0=sm[:, 64:72], scalar1=EPS, scalar2=None, op0=ALU.add),
                  reads=["gvar"], writes=["gve"])
                A("pool", lambda e: e.tensor_tensor(out=sm[:, 80:88], in0=sm[:, 72:80], in1=nh[:], op=ALU.pow),
                  reads=["gve", "nh"], writes=["grstd"])
                A("dve", lambda e: e.tensor_tensor(out=gv3, in0=gv3, in1=sm[:, 48:56].unsqueeze(2).to_broadcast([128, 8, 64]), op=ALU.subtract),
                  reads=[tgv, "gmean"], writes=[tgv])
                A("dve", lambda e: e.tensor_tensor(out=gv3, in0=gv3, in1=sm[:, 80:88].unsqueeze(2).to_broadcast([128, 8, 64]), op=ALU.mult),
                  reads=[tgv, "grstd"], writes=[tgv])
                A("dve", lambda e: e.tensor_tensor(out=gv[:], in0=gv[:], in1=sgug[:], op=ALU.mult), reads=[tgv, "sgug"], writes=[tgv])
                A("dve", lambda e: e.tensor_tensor(out=vn[:], in0=gv[:], in1=sgub[:], op=ALU.add), reads=[tgv, "sgub"], writes=["vn"])
                for g in range(8):
                    A("pe", lambda e, g=g: e.matmul(B[0][:, g * 64:(g + 1) * 64], lhsT=wsT[:, g, :], rhs=vn[:, g * 64:(g + 1) * 64],
                                                    start=True, stop=True), reads=["vn", "wsT"], writes=["B0"])
                A("dve", lambda e: e.tensor_tensor(out=gv3, in0=B[0][:].rearrange("p (g d) -> p g d", g=8),
                                                   in1=bs[:].unsqueeze(2).to_broadcast([128, 8, 64]), op=ALU.add),
                  reads=["B0", "bs", "vn"], writes=[tgv])
                A("dve", lambda e: e.tensor_tensor(out=gv[:], in0=gv[:], in1=gu[:], op=ALU.mult), reads=[tgv, tgu], writes=[tgv])
                rms_to(gv[:], [tgv], mixed[:, rq, 512:1024], "mixS%d" % rq, 88, junk[:])

            def step_B(m):
                rq = m % RQ
                W = win(m)
                nW = len(W)
                edge = (sbk == 0 and m < 2)
                po = [B[6][:, 0:260].rearrange("p (h d) -> p h d", h=4), B[7][:, 0:260].rearrange("p (h d) -> p h d", h=4)]

                def scores(h, part):
                    c, pb = h // 2, (h % 2) * 64
                    for i, kvj in enumerate(W):
                        if (i < 4) != (part == 0):
                            continue
                        if i < 4:
                            oap, tok = B[4][:, i * 128:(i + 1) * 128], "B4"
                        else:
                            oap, tok = B[5][:, 0:128], "B5"
                        A("pe", lambda e, oap=oap, kvj=kvj, pb=pb, c=c: e.matmul(
                            oap, lhsT=kT[pb:pb + 64, kvj % RK, c, :], rhs=qT[pb:pb + 64, rq, c, :], start=True, stop=True),
                          reads=["kT%d" % (kvj % RK), "qT%d" % rq], writes=[tok])

                def softmax_pv(h, mid):
                    hp = h % 2
                    mb = 4
                    ei = 0
                    if edge:
                        idx = m * 8 + h
                        ei = idx % 2
                        A("sp", lambda e, idx=idx, ei=ei: e.dma_start(out=est[ei][:], in_=bm_edge_d[l, idx]), writes=["est0"],
                          kind="d", dsem="ee0")
                        A("act", lambda e, ei=ei: e.activation(out=eed[ei][:], in_=est[ei][:], func=AF.Exp), reads=["est0"],
                          writes=["eed0"])
                    A("act", lambda e, hp=hp, mb=mb: e.activation(out=pexp[hp][:, 0:512], in_=B[mb][:], func=AF.Exp), reads=["B%d" % mb],
                      writes=["pexpa%d" % hp])
                    ptoks = ["pexpa%d" % hp]
                    if nW > 4:
                        A("act", lambda e, hp=hp: e.activation(out=pexp[hp][:, 512:640], in_=B[5][:, 0:128], func=AF.Exp),
                          reads=["B5"], writes=["pexpb%d" % hp])
                        ptoks.append("pexpb%d" % hp)
                    mid()
                    if edge:
                        A("pool", lambda e, hp=hp, ei=ei: e.tensor_tensor(out=pTs[hp][:, 0:512], in0=pexp[hp][:, 0:512],
                                                                           in1=eed[ei][:, 0:512], op=ALU.mult),
                          reads=ptoks + ["eed0"], writes=["pTs%d" % hp])
                    else:
                        A("pool" if hp == 0 else "dve", lambda e, hp=hp, h=h: e.tensor_tensor(out=pTs[hp][:, 0:640],
                                                                         in0=pexp[hp][:, 0:640],
                                                                         in1=E_int[:, h, :, :].rearrange("p s q -> p (s q)"), op=ALU.mult),
                          reads=ptoks + E_INT_TOK, writes=["pTs%d" % hp])
                    pbk = 6 + h // 4
                    for i, kvj in enumerate(W):
                        A("pe", lambda e, i=i, kvj=kvj, hp=hp, h=h: e.matmul(po[h // 4][:, h % 4, :], lhsT=pTs[hp][:, i * 128:(i + 1) * 128],
                                                                               rhs=Vt[:, kvj % RK, h, :], start=(i == 0), stop=(i == nW - 1)),
                          reads=["pTs%d" % hp, "V%d" % (kvj % RK)], writes=["B%d" % pbk])

                scores(0, 0)
                scores(0, 1)
                for h in range(8):
                    if h + 1 < 8:
                        softmax_pv(h, lambda h=h: (scores(h + 1, 0), scores(h + 1, 1)))
                    else:
                        softmax_pv(h, lambda: None)
                for hh in range(2):
                    A("dve", lambda e, hh=hh: e.reciprocal(out=sm[:, 96 + hh * 4:100 + hh * 4], in_=po[hh][:, :, 64]), reads=["B%d" % (6 + hh)],
                      writes=["rec%d" % hh])
                    A("dve", lambda e, hh=hh: e.tensor_tensor(out=a_o[:, hh * 256:(hh + 1) * 256].rearrange("p (h d) -> p h d", h=4),
                                                              in0=po[hh][:, :, 0:64],
                                                              in1=sm[:, 96 + hh * 4:100 + hh * 4].unsqueeze(2).to_broadcast([128, 4, 64]), op=ALU.mult),
                      reads=["B%d" % (6 + hh), "rec%d" % hh], writes=["a_o%d" % hh])
                rms_to(a_o[:], ["a_o0", "a_o1"], mixed[:, rq, 0:512], "mixA%d" % rq, 104, junk[:])

            def step_C(m):
                rq = m % RQ
                t = "xres%d" % m

                def evac_m(pv, tk):
                    A("dve", lambda e: e.tensor_tensor(out=mT[:], in0=pv, in1=mixg[:].unsqueeze(2).to_broadcast([128, 8, 128]), op=ALU.mult),
                      reads=[tk, "mixg"], writes=["mT"])
                transpose_to(mixed[:, rq, :], ["mixA%d" % rq, "mixS%d" % rq], evac_m, 3)
                for half in range(2):
                    for k in range(8):
                        A("pe", lambda e, half=half, k=k: e.matmul(B[3][:], lhsT=mT[:, k, :], rhs=w_out[:, k, half * 512:(half + 1) * 512],
                                                                   start=(k == 0), stop=(k == 7)), reads=["mT", "w_out"], writes=["B3"])
                    A("dve", lambda e, half=half: e.scalar_tensor_tensor(out=x_res[:, m, half * 512:(half + 1) * 512],
                                                                         in0=x_res[:, m, half * 512:(half + 1) * 512], scalar=ALPHA,
                                                                         in1=B[3][:], op0=ALU.mult, op1=ALU.add),
                      reads=[t, "B3"], writes=[t])
                ln_tile(m, ln1g[:], ln1b[:], ["ln1g", "ln1b"])
                r2 = m % 2
                A("act", lambda e: e.copy(out=xbc[:], in_=x_res[:, m, :]), reads=[t], writes=["xbc"])
                transpose_to(xbc[:], ["xbc"],
                             lambda pv, tk: A("act", lambda e: e.copy(out=x1T[:, :, m * 128:(m + 1) * 128], in_=pv), reads=[tk], writes=["x1T%d" % m]), 3)
                for k in range(8):
                    A("pe", lambda e, k=k: e.matmul(B[3][:, 0:20], lhsT=x1T[:, k, m * 128:(m + 1) * 128], rhs=wr[:, k, :],
                                                    start=(k == 0), stop=(k == 7)), reads=["x1T%d" % m, "wr"], writes=["B3"])
                rs = rscr[m % 2]
                A("dve", lambda e: e.tensor_tensor(out=rs[:, 0:20], in0=B[3][:, 0:20], in1=br[:], op=ALU.add), reads=["B3", "br"],
                  writes=["rscr%d" % (m % 2)])

            def router_chain(m):
                rs = rscr[m % 2]
                tk = ["rscr%d" % (m % 2)]
                R = lambda a, b: rs[:, a:b]
                D = lambda fn: A("dve", fn, reads=tk, writes=tk)
                D(lambda e: e.tensor_reduce(out=R(20, 21), in_=R(0, 4), axis=AX.X, op=ALU.max))
                D(lambda e: e.tensor_scalar(out=R(21, 22), in0=R(20, 21), scalar1=-1.0, scalar2=None, op0=ALU.mult))
                A("act", lambda e: e.activation(out=R(24, 28), in_=R(0, 4), func=AF.Exp, bias=R(21, 22), scale=1.0, accum_out=R(22, 23)),
                  reads=tk, writes=tk)
                D(lambda e: e.reciprocal(out=R(23, 24), in_=R(22, 23)))
                D(lambda e: e.tensor_scalar(out=R(28, 32), in0=R(0, 4), scalar1=R(20, 21), scalar2=None, op0=ALU.is_equal))
                D(lambda e: e.tensor_tensor(out=rs[:, 32:48].rearrange("p (g e) -> p g e", g=4),
                                            in0=rs[:, 4:20].rearrange("p (g e) -> p g e", g=4),
                                            in1=R(28, 32).unsqueeze(2).to_broadcast([128, 4, 4]), op=ALU.mult))
                D(lambda e: e.tensor_reduce(out=R(48, 52), in_=rs[:, 32:48].rearrange("p (g e) -> p e g", g=4), axis=AX.X, op=ALU.add))
                D(lambda e: e.tensor_reduce(out=R(52, 53), in_=R(48, 52), axis=AX.X, op=ALU.max))
                D(lambda e: e.tensor_scalar(out=R(56, 60), in0=R(48, 52), scalar1=R(52, 53), scalar2=None, op0=ALU.is_equal))
                D(lambda e: e.scalar_tensor_tensor(out=R(60, 64), in0=R(56, 60), scalar=-1e30, in1=R(48, 52), op0=ALU.mult, op1=ALU.add))
                D(lambda e: e.tensor_reduce(out=R(53, 54), in_=R(60, 64), axis=AX.X, op=ALU.max))
                D(lambda e: e.tensor_scalar(out=R(64, 68), in0=R(60, 64), scalar1=R(53, 54), scalar2=None, op0=ALU.is_equal))
                D(lambda e: e.tensor_tensor(out=R(54, 55), in0=R(53, 54), in1=R(52, 53), op=ALU.subtract))
                A("act", lambda e: e.activation(out=R(55, 56), in_=R(54, 55), func=AF.Exp), reads=tk, writes=tk)
                D(lambda e: e.tensor_scalar(out=R(68, 69), in0=R(55, 56), scalar1=1.0, scalar2=None, op0=ALU.add))
                D(lambda e: e.reciprocal(out=R(69, 70), in_=R(68, 69)))
                D(lambda e: e.tensor_tensor(out=R(70, 71), in0=R(69, 70), in1=R(55, 56), op=ALU.mult))
                D(lambda e: e.tensor_scalar(out=R(72, 76), in0=R(56, 60), scalar1=R(69, 70), scalar2=None, op0=ALU.mult))
                D(lambda e: e.scalar_tensor_tensor(out=R(76, 80), in0=R(64, 68), scalar=R(70, 71), in1=R(72, 76), op0=ALU.mult, op1=ALU.add))
                D(lambda e: e.tensor_scalar(out=R(80, 84), in0=R(76, 80), scalar1=R(23, 24), scalar2=1.0 / ALPHA, op0=ALU.mult, op1=ALU.mult))
                A("dve", lambda e: e.tensor_tensor(out=comb[:, m, :].rearrange("p (g e) -> p g e", g=4),
                                                   in0=R(28, 32).unsqueeze(2).to_broadcast([128, 4, 4]),
                                                   in1=R(80, 84).unsqueeze(1).to_broadcast([128, 4, 4]), op=ALU.mult), reads=tk, writes=["comb%d" % m])

            aseq = []
            for j in range(n_kv):
                mq = j - q_off - 2
                if 0 <= mq < n_q:
                    aseq.append(("q", mq))
                aseq.append(("kv", j))
            for mq in range(n_q):
                if ("q", mq) not in aseq:
                    aseq.append(("q", mq))
            apos = [0]

            def emit_A_until(kv_need, q_need, lists=None):
                def have():
                    donev = [u for u in aseq[:apos[0]]]
                    return (("kv", kv_need) in donev or kv_need < 0) and (("q", q_need) in donev or q_need < 0)
                while not have():
                    kind, idx = aseq[apos[0]]
                    apos[0] += 1
                    if lists is None:
                        (A_kv if kind == "kv" else A_q)(idx)
                    else:
                        S.capture_begin()
                        (A_kv if kind == "kv" else A_q)(idx)
                        lists[kind].extend(S.capture_end())

            def cap(fn):
                S.capture_begin()
                fn()
                return S.capture_end()

            emit_A_until(max(win(0)), 0)
            prologue = S.capture_end()
            in_prologue[0] = False
            S.commit_interleaved([pending_ln2[0] if pending_ln2 is not None else [], prologue])
            for m in range(n_q):
                lc = cap(lambda: (step_C(m - 1) if m >= 1 else None))
                la = {"kv": [], "q": []}
                if m + 1 < n_q:
                    emit_A_until(max(win(m + 1)), m + 1, la)
                else:
                    emit_A_until(n_kv - 1, n_q - 1, la)
                ly = cap(lambda: step_B(m))
                lz = cap(lambda: (router_chain(m - 2) if m >= 2 else None))
                lg = cap(lambda: sgu_chain(m))
                S.commit_interleaved([ly, lc, la["q"], la["kv"], lz, lg])
            def expert_dma(ex):
                ebi = ex % 2
                A("pool", lambda e: e.dma_start(out=wg_b[ebi], in_=wg_d[l, ex].rearrange("(k p) f -> p k f", p=128)),
                  writes=["wg%d" % ebi], kind="d", dsem="ewg%d" % ebi)
                A("pool", lambda e: e.dma_start(out=wu_b[ebi], in_=wu_d[l, ex].rearrange("(k p) f -> p k f", p=128)),
                  writes=["wu%d" % ebi], kind="d", dsem="ewu%d" % ebi)
                A("pool", lambda e: e.dma_start(out=wd_b[ebi], in_=wd_d[l, ex].rearrange("(c p) n -> p c n", p=128)),
                  writes=["wd%d" % ebi], kind="d", dsem="ewd%d" % ebi)

            def early_experts():
                A("pool", lambda e: e.memset(sm[:, 127:128], 0.0),
                  writes=[t for t in RING_TOK if t not in ("mixA2", "mixA3", "mixS2", "mixS3")] + EXP_TOK)
                expert_dma(0)
                expert_dma(1)

            lx = cap(lambda: step_C(n_q - 1))
            lz = cap(lambda: router_chain(n_q - 2))
            le = cap(early_experts)
            S.commit_interleaved([lx, lz, le])
            router_chain(n_q - 1)
            assert apos[0] == len(aseq)

            n_tg = n_q // 4
            yrot = [5, 6, 7, 0]
            yi = [0]

            def emit_down(ex, tg, ri, ebi):
                for t in range(4):
                    tile = tg * 4 + t
                    tt = "xres%d" % tile
                    for half in range(2):
                        yb = yrot[yi[0] % 4]
                        yi[0] += 1
                        for fc in range(2):
                            A("pe", lambda e, fc=fc, t=t, half=half, yb=yb, ri=ri, ebi=ebi: e.matmul(
                                B[yb][:], lhsT=act_b[ri][:, fc, t * 128:(t + 1) * 128], rhs=wd_b[ebi][:, fc, half * 512:(half + 1) * 512],
                                start=(fc == 0), stop=(fc == 1)), reads=["act%d_0" % ri, "act%d_1" % ri, "wd%d" % ebi],
                              writes=["B%d" % yb])
                        A("dve", lambda e, tile=tile, half=half, yb=yb, ex=ex: e.scalar_tensor_tensor(
                            out=x_res[:, tile, half * 512:(half + 1) * 512], in0=B[yb][:], scalar=comb[:, tile, ex:ex + 1],
                            in1=x_res[:, tile, half * 512:(half + 1) * 512], op0=ALU.mult, op1=ALU.add),
                          reads=["B%d" % yb, tt, "comb%d" % tile], writes=[tt])

            A("pool", lambda e: e.memset(sm[:, 126:127], 0.0), writes=["mixA2", "mixA3", "mixS2", "mixS3"] + SGACT_TOK)
            pending = None
            hcount = 0
            for ex in range(16):
                ebi = ex % 2
                if prefetch is not None and ex in (2, 5, 8, 11, 13):
                    mixer_weight_chunk(prefetch, (2, 5, 8, 11, 13).index(ex))
                if ex >= 2:
                    expert_dma(ex)
                for tg in range(n_tg):
                    ri = (ex * n_tg + tg) % 2
                    x1toks = ["x1T%d" % (tg * 4 + t) for t in range(4)]
                    for fc in range(2):
                        hs = hcount % 2
                        hcount += 1
                        gb, ub = (1, 2) if hs == 0 else (3, 4)
                        for k in range(8):
                            A("pe", lambda e, fc=fc, k=k, ebi=ebi, tg=tg, gb=gb: e.matmul(B[gb][:], lhsT=wg_b[ebi][:, k, fc * 128:(fc + 1) * 128],
                                                                                          rhs=x1T[:, k, tg * 512:(tg + 1) * 512], start=(k == 0), stop=(k == 7)),
                              reads=x1toks + ["wg%d" % ebi], writes=["B%d" % gb])
                        for k in range(8):
                            A("pe", lambda e, fc=fc, k=k, ebi=ebi, tg=tg, ub=ub: e.matmul(B[ub][:], lhsT=wu_b[ebi][:, k, fc * 128:(fc + 1) * 128],
                                                                                          rhs=x1T[:, k, tg * 512:(tg + 1) * 512], start=(k == 0), stop=(k == 7)),
                              reads=x1toks + ["wu%d" % ebi], writes=["B%d" % ub])
                        A("act", lambda e, hs=hs, gb=gb: e.activation(out=sg_b[hs][:, 0, :], in_=B[gb][:], func=AF.Silu),
                          reads=["B%d" % gb], writes=["sg%d_0" % hs])
                        A("dve", lambda e, fc=fc, ri=ri, hs=hs, ub=ub: e.tensor_tensor(out=act_b[ri][:, fc, :], in0=sg_b[hs][:, 0, :], in1=B[ub][:], op=ALU.mult),
                          reads=["sg%d_0" % hs, "B%d" % ub], writes=["act%d_%d" % (ri, fc)])
                        if fc == 0 and pending is not None:
                            emit_down(*pending)
                            pending = None
                    pending = (ex, tg, ri, ebi)
            emit_down(*pending)

            S.capture_begin()
            A("sp", lambda e: e.dma_start(out=ln1g[:], in_=ln2g_d[l].partition_broadcast(128)), writes=["ln1g"], kind="d", dsem="c11")
            A("sp", lambda e: e.dma_start(out=ln1b[:], in_=ln2b_d[l].partition_broadcast(128)), writes=["ln1b"], kind="d", dsem="c12")
            ends = {}
            for m in range(n_q):
                ln_tile(m, ln1g[:], ln1b[:], ["ln1g", "ln1b"], eps=EPS / (ALPHA * ALPHA))
                gt = sbk * 8 + m
                dap = dst_tile_ap(gt)
                A("sp", lambda e, m=m, dap=dap: e.dma_start(out=dap, in_=x_res[:, m, :]), reads=["xres%d" % m],
                  writes=[dst_tok(gt)], kind="d", dsem="xw%d" % gt)
                if (not last_layer) and gt >= 14 and use_cc:
                    ci = cc_in.ap()
                    A("sp", lambda e, m=m, gt=gt, ci=ci: e.dma_start(out=ci[(gt - 14) * 128:(gt - 13) * 128, :], in_=x_res[:, m, :]),
                      reads=["xres%d" % m], writes=["ccin%d" % (gt - 14)], kind="d", dsem="xw%d" % (gt + 2))
                ends[m] = len(S._cap)
            ln_next = l if sbk == 0 else (l + 1 if not last_layer else None)
            if ln_next is not None:
                A("sp", lambda e: e.dma_start(out=ln1g[:], in_=ln1g_d[ln_next].partition_broadcast(128)), writes=["ln1g"], kind="d", dsem="c8")
                A("sp", lambda e: e.dma_start(out=ln1b[:], in_=ln1b_d[ln_next].partition_broadcast(128)), writes=["ln1b"], kind="d", dsem="c9")
            if (not last_layer) and sbk == 1 and use_cc:
                A("pool", lambda e: e.collective_compute("AllGather", ALU.bypass, replica_groups=[[0, 1], [2, 3], [4, 5], [6, 7]],
                                                         ins=[cc_in.ap().opt()], outs=[cc_out.ap().opt()]),
                  reads=["ccin0", "ccin1"], writes=["gath"], kind="cc", dsem="cc%d" % l)
            return (S.capture_end(), ends)

        pend = [None]
        for l in range(depth):
            last = (l == depth - 1)
            if l == 0:
                src = lambda t: (x_d[t * 128:(t + 1) * 128, :], "x_in")
            else:
                sbuf_ = bufs[(l - 1) % 2]
                src = (lambda sb_, bi: (lambda t: (sb_[t * 128:(t + 1) * 128, :], "xd%d_%d" % (bi, t))))(sbuf_, (l - 1) % 2)
            if last:
                dst = lambda t: out_d[t * 128:(t + 1) * 128, :]
                dtok = lambda t: "outd_%d" % t
            else:
                dbuf_ = bufs[l % 2]
                dst = (lambda db_: (lambda t: db_[t * 128:(t + 1) * 128, :]))(dbuf_)
                dtok = (lambda bi: (lambda t: "xd%d_%d" % (bi, t)))(l % 2)
            layer_consts(l)
            if l == 0:
                for q4 in range(5):
                    mixer_weight_chunk(0, q4)
            for sbk in range(2):
                pend[0] = subblock(l, sbk, src, dst, dtok, last, (l + 1) if (sbk == 1 and not last) else None, pend[0])
        S.commit_interleaved([pend[0][0]])
        final = ["xw%d" % t for t in range(16)]
        S.emit(sems, dsems, final_wait=final)
    return nc


GRID_W = 64
SEQ = 4096


def _token_map(flip):
    u = np.arange((N_OWN + 2) * 128)
    return (SEQ - 1 - u) if flip else u


def _bias_tables(rel_bias, flip):
    gmap = _token_map(flip)
    H = rel_bias.shape[0]

    def slot(m, kv):
        gq = gmap[m * 128:(m + 1) * 128]
        gk = gmap[kv * 128:(kv + 1) * 128]
        rq, cq = gq // GRID_W, gq % GRID_W
        rk, ck = gk // GRID_W, gk % GRID_W
        rs = np.clip(rq - 4, 0, 56)
        cs = np.clip(cq - 8, 0, 48)
        ok = (rk[:, None] >= rs[None, :]) & (rk[:, None] < rs[None, :] + 8)
        ok &= (ck[:, None] >= cs[None, :]) & (ck[:, None] < cs[None, :] + 16)
        dr = np.clip(rk[:, None] - rq[None, :] + 7, 0, 14)
        dc = np.clip(ck[:, None] - cq[None, :] + 15, 0, 30)
        tab = rel_bias[:, dr, dc]
        return np.where(ok[None], tab, np.float32(NEG)).astype(np.float32)

    interior = np.empty((128, 5, H, 128), np.float32)
    for i in range(5):
        interior[:, i] = slot(4, 2 + i).transpose(1, 0, 2)
    edge = np.empty((2, H, 128, 4, 128), np.float32)
    for m in range(2):
        for i in range(4):
            edge[m, :, :, i, :] = slot(m, i)
    return interior.reshape(128, 5 * H * 128), edge.reshape(2 * H, 128, 512)


_PROG = {}


def _get_prog(depth):
    if depth not in _PROG:
        _PROG[depth] = build_program(depth)
    return _PROG[depth]


def kernel(x, w_in, w_out, na_rel_bias, sgu_ln_g, sgu_ln_b, sgu_w, sgu_b, mix_norm_g,
           ln1_g, ln1_b, router_group_w, router_group_b, router_expert_w, router_expert_b,
           expert_w_gate, expert_w_up, expert_w_down, ln2_g, ln2_b):
    f = lambda a: np.ascontiguousarray(np.asarray(a, dtype=np.float32))
    x = f(x)
    Bsz, T, D = x.shape
    depth = w_in.shape[0]
    nc = _get_prog(depth)
    ident = np.eye(128, dtype=np.float32)
    jm = np.ascontiguousarray(ident[::-1])
    shared = dict(
        w_in=f(w_in), w_out=f(w_out),
        sgu_g=f(sgu_ln_g).reshape(depth, 1, 512), sgu_b=f(sgu_ln_b).reshape(depth, 1, 512),
        mixg=f(np.asarray(mix_norm_g).reshape(depth, 8, 128).transpose(0, 2, 1)),
        ln1g=f(ln1_g).reshape(depth, 1, 1024), ln1b=f(ln1_b).reshape(depth, 1, 1024),
        ln2g=f(ln2_g).reshape(depth, 1, 1024), ln2b=f(ln2_b).reshape(depth, 1, 1024),
        wr=f(np.concatenate([np.asarray(router_group_w), np.asarray(router_expert_w)], axis=2)),
        br=f(np.concatenate([np.asarray(router_group_b), np.asarray(router_expert_b)], axis=1)).reshape(depth, 1, 20),
        wg=f(expert_w_gate), wu=f(expert_w_up), wd=f(expert_w_down), ident=ident, jmat=jm)
    sw = np.asarray(sgu_w, dtype=np.float32)
    sbias = np.asarray(sgu_b, dtype=np.float32)
    rb = np.asarray(na_rel_bias, dtype=np.float32)
    per_type = []
    for flip in (False, True):
        swl = sw[:, :, ::-1, ::-1] if flip else sw
        sbl = sbias[:, :, ::-1] if flip else sbias
        tabs = [_bias_tables(rb[l], flip) for l in range(depth)]
        selv = np.zeros((128, 2), np.float32)
        selv[:, 0 if flip else 1] = 1.0
        per_type.append(dict(wsT=f(swl.transpose(0, 3, 1, 2).reshape(depth, 128, 1024)), bs=f(sbl.transpose(0, 2, 1)),
                             bm_int=f(np.stack([t[0] for t in tabs])), bm_edge=f(np.stack([t[1] for t in tabs])), sel=selv))
    in_maps = []
    for c in range(8):
        b, flip = c // 2, c % 2
        gmap = _token_map(bool(flip))
        d = dict(shared)
        d.update(per_type[flip])
        d["x"] = f(x[b, gmap])
        in_maps.append(d)
    res = run_bass_kernel_spmd(nc, in_maps, core_ids=list(range(8)))
    out = np.empty_like(x)
    for c in range(8):
        b, flip = c // 2, c % 2
        gmap = _token_map(bool(flip))
        out[b, gmap[:N_OWN * 128]] = res.results[c]["out"]
    return out
```

```python
import contextlib
import numpy as np
import concourse.bass as bass
import concourse.mybir as mybir
from concourse.bass_utils import run_bass_kernel_spmd

F32 = mybir.dt.float32
BF16 = mybir.dt.bfloat16
AF = mybir.ActivationFunctionType
ALU = mybir.AluOpType
AX = mybir.AxisListType

ALPHA = float(8 ** 0.25)
EPS = 1e-5
NEG = -30000.0
ENGS = ("pe", "act", "dve", "pool", "sp")


class Sched:
    def __init__(self, nc):
        self.nc = nc
        self.ops = []
        self.last_writer = {}
        self.readers = {}

    def capture_begin(self):
        self._cap = []

    def capture_end(self):
        c, self._cap = self._cap, None
        return c

    def commit_interleaved(self, lists, bias=None):
        has0 = bool(lists) and bool(lists[0])
        if bias is None:
            bias = [0.0] * len(lists)
        bias = [b for l, b in zip(lists, bias) if l]
        lists = [l for l in lists if l]
        pos = [0] * len(lists)
        while True:
            best, bf = None, None
            for i, l in enumerate(lists):
                if pos[i] < len(l):
                    f = pos[i] / len(l) - bias[i]
                    if bf is None or f < bf:
                        best, bf = i, f
            if best is None:
                break
            item = lists[best][pos[best]]
            if has0 and item[6] is not None and best != 0 and pos[0] < min(item[6], len(lists[0])):
                best = 0
                item = lists[0][pos[0]]
            self.add(*item[:6])
            pos[best] += 1
            while len(item) > 7 and item[7] and pos[best] < len(lists[best]):
                item = lists[best][pos[best]]
                self.add(*item[:6])
                pos[best] += 1

    def add(self, eng, fn, reads=(), writes=(), kind="c", dsem=None, req=None, glue=False):
        if getattr(self, "_cap", None) is not None:
            self._cap.append((eng, fn, tuple(reads), tuple(writes), kind, dsem, req, glue))
            return -1
        idx = len(self.ops)
        deps = {}
        for r in reads:
            w = self.last_writer.get(r)
            if w is not None:
                deps[w] = "raw"
        for wr in writes:
            w = self.last_writer.get(wr)
            if w is not None and w not in deps:
                deps[w] = "waw"
            rd = self.readers.get(wr)
            if rd:
                for j in rd[0].values():
                    if j not in deps:
                        deps[j] = "war"
                for j in rd[1]:
                    if j not in deps:
                        deps[j] = "war"
        for r in reads:
            rd = self.readers.setdefault(r, ({}, []))
            if kind != "c":
                rd[1].append(idx)
            else:
                rd[0][eng] = idx
        for wr in writes:
            self.last_writer[wr] = idx
            self.readers[wr] = ({}, [])
        self.ops.append(dict(eng=eng, fn=fn, deps=deps, kind=kind, dsem=dsem,
                             signal=False, sigval=None))
        return idx

    def emit(self, sems, dsems, final_wait=()):
        ops = self.ops
        need = [[] for _ in ops]
        for i, op in enumerate(ops):
            for j, typ in op["deps"].items():
                src = ops[j]
                if src["kind"] == "c" and op["kind"] == "c" and src["eng"] == op["eng"]:
                    if op["eng"] == "pe" or typ != "raw":
                        continue
                need[i].append(j)
                src["signal"] = True
        cnt = {e: 0 for e in ENGS}
        dcnt = {k: 0 for k in dsems}
        for op in ops:
            if op["kind"] in ("d", "cc"):
                dcnt[op["dsem"]] += 16 if op["kind"] == "d" else 1
                op["sigval"] = (op["dsem"], dcnt[op["dsem"]])
            elif op["signal"]:
                cnt[op["eng"]] += 1
                op["sigval"] = (op["eng"], cnt[op["eng"]])

        def run(engname, eng):
            waited = {}
            for i, op in enumerate(ops):
                if op["eng"] != engname:
                    continue
                req = {}
                for j in need[i]:
                    k, v = ops[j]["sigval"]
                    if v > req.get(k, 0):
                        req[k] = v
                for k, v in req.items():
                    if waited.get(k, 0) >= v:
                        continue
                    waited[k] = v
                    eng.wait_ge(sems[k] if k in sems else dsems[k], v)
                ins = op["fn"](eng)
                if op["kind"] == "d":
                    ins.then_inc(dsems[op["dsem"]], 16)
                elif op["kind"] == "cc":
                    ins.then_inc(dsems[op["dsem"]], 1)
                elif op["signal"]:
                    ins.then_inc(sems[op["eng"]], 1)
            if engname == "sp":
                for k in final_wait:
                    eng.wait_ge(dsems[k], dcnt[k])

        with self.nc.Block() as block:
            @block.tensor
            def _(e):
                run("pe", e)

            @block.scalar
            def _(e):
                run("act", e)

            @block.vector
            def _(e):
                run("dve", e)

            @block.gpsimd
            def _(e):
                run("pool", e)

            @block.sync
            def _(e):
                run("sp", e)


N_OWN = 16
N_Q = 8


def build_program(depth=4, use_cc=True):
    n_q = N_Q
    NT = n_q * 128
    RK = 8
    RQ = 4
    nc = bass.Bass("TRN2", target_bir_lowering=False)

    def din(name, shape):
        return nc.dram_tensor(name, shape, F32, kind="ExternalInput").ap()

    L = depth
    x_d = din("x", [(N_OWN + 2) * 128, 1024])
    w_in_d = din("w_in", [L, 1024, 2560])
    w_out_d = din("w_out", [L, 1024, 1024])
    bm_int_d = din("bm_int", [L, 128, 5 * 1024])
    bm_edge_d = din("bm_edge", [L, 16, 128, 512])
    wsT_d = din("wsT", [L, 128, 1024])
    bs_d = din("bs", [L, 128, 8])
    sgug_d = din("sgu_g", [L, 1, 512])
    sgub_d = din("sgu_b", [L, 1, 512])
    mixg_d = din("mixg", [L, 128, 8])
    ln1g_d = din("ln1g", [L, 1, 1024])
    ln1b_d = din("ln1b", [L, 1, 1024])
    ln2g_d = din("ln2g", [L, 1, 1024])
    ln2b_d = din("ln2b", [L, 1, 1024])
    wr_d = din("wr", [L, 1024, 20])
    br_d = din("br", [L, 1, 20])
    wg_d = din("wg", [L, 16, 1024, 256])
    wu_d = din("wu", [L, 16, 1024, 256])
    wd_d = din("wd", [L, 16, 256, 1024])
    ident_d = din("ident", [128, 128])
    jmat_d = din("jmat", [128, 128])
    sel_d = din("sel", [128, 2])
    out_d = nc.dram_tensor("out", [N_OWN * 128, 1024], F32, kind="ExternalOutput").ap()
    bufs = [nc.dram_tensor("xbuf%d" % i, [N_OWN * 128, 1024], F32).ap() for i in range(2)]
    cc_in = nc.dram_tensor("cc_in", [256, 1024], F32)
    cc_out = nc.dram_tensor("cc_out", [512, 1024], F32)

    with contextlib.ExitStack() as es:
        def sb(name, shape, dt=F32):
            return es.enter_context(nc.sbuf_tensor("sb_" + name, shape, dt))

        def ps(name, shape, dt=F32):
            return es.enter_context(nc.psum_tensor("ps_" + name, shape, dt))

        x_res = sb("x_res", [128, n_q, 1024])
        w_in = sb("w_in", [128, 8, 2560], BF16)
        w_out = sb("w_out", [128, 8, 1024], BF16)
        E_int = sb("E_int", [128, 5, 8, 128], BF16)
        est = [sb("est0", [128, 512])] * 2
        eed = [sb("eed0", [128, 512], BF16)] * 2
        xst = [sb("xst%d" % i, [128, 1024]) for i in range(2)]
        xb = [sb("xb%d" % i, [128, 1024], BF16) for i in range(2)]
        xT = [sb("xT%d" % i, [128, 8, 128], BF16) for i in range(3)]
        arena = sb("arena", [128, 15360], BF16)
        kT = arena[:, 0:4096].rearrange("p (r c n) -> p r c n", r=RK, c=4)
        Vt = arena[:, 4096:8256].rearrange("p (r h d) -> p r h d", r=RK, h=8)
        qT = arena[:, 8256:10304].rearrange("p (r c n) -> p r c n", r=RQ, c=4)
        gu_ = [sb("gu%d" % i, [128, 512]) for i in range(2)]
        gv_ = [sb("gv%d" % i, [128, 512]) for i in range(2)]
        sq = sb("sq", [128, 512])
        vn = sb("vn", [128, 512], BF16)
        mixed = arena[:, 10304:14400].rearrange("p (r n) -> p r n", r=RQ)
        pexp = [sb("pexp%d" % i, [128, 640], BF16) for i in range(2)]
        pTs = [sb("pTs%d" % i, [128, 640], BF16) for i in range(2)]
        a_o = sb("a_o", [128, 512])
        mT = sb("mT", [128, 8, 128], BF16)
        kb_ = sb("kb_", [128, 512], BF16)
        qb_ = sb("qb_", [128, 512], BF16)
        x1T = sb("x1T", [128, 8, NT], BF16)
        ln1g = sb("ln1g", [128, 1024])
        ln1b = sb("ln1b", [128, 1024])
        sgug = sb("sgug", [128, 512])
        sgub = sb("sgub", [128, 512])
        wsT = sb("wsT", [128, 8, 128], BF16)
        wr = sb("wr", [128, 8, 20], BF16)
        br = sb("br", [128, 20])
        bs = sb("bs", [128, 8])
        mixg = sb("mixg", [128, 8])
        ident = sb("ident", [128, 128], BF16)
        jmat = sb("jmat", [128, 128], BF16)
        sel = sb("sel", [128, 2])
        nh = sb("nh", [128, 8])
        comb = sb("comb", [128, n_q, 16])
        sm = sb("sm", [128, 192])
        junk = sb("junk", [128, 512], BF16)
        xbc = sb("xbc", [128, 1024], BF16)
        rscr = [sb("rscr%d" % i, [128, 96]) for i in range(2)]
        wg_b, wu_b, wd_b = [], [], []
        for i in range(2):
            o = i * 6144
            wg_b.append(arena[:, o:o + 2048].rearrange("p (k f) -> p k f", k=8))
            wu_b.append(arena[:, o + 2048:o + 4096].rearrange("p (k f) -> p k f", k=8))
            wd_b.append(arena[:, o + 4096:o + 6144].rearrange("p (c n) -> p c n", c=2))
        sg_b = [arena[:, 12288 + i * 512:12288 + (i + 1) * 512].rearrange("p (c n) -> p c n", c=1) for i in range(2)]
        act_b = [arena[:, 13312 + i * 1024:13312 + (i + 1) * 1024].rearrange("p (c n) -> p c n", c=2) for i in range(2)]

        B = [ps("B%d" % i, [128, 512]) for i in range(8)]
        pTv = [B[i][:].bitcast(BF16).rearrange("p (k n) -> p k n", k=8) for i in range(8)]

        sems = {e: es.enter_context(nc.semaphore("s_" + e)) for e in ENGS}
        dnames = (["c%d" % i for i in range(16)] + ["xl%d" % i for i in range(12)] + ["xw%d" % i for i in range(18)]
                  + ["ee0", "ee1", "ewg0", "ewg1", "ewu0", "ewu1", "ewd0", "ewd1", "st", "win0", "win1", "win2", "win3", "ei0", "ei1", "cc0", "cc1", "cc2", "cc3", "ga", "gb"])
        dsems = {k: es.enter_context(nc.semaphore("d_" + k)) for k in dnames}
        S = Sched(nc)
        A = S.add

        W_IN_TOK = ["w_in_l%d" % i for i in range(4)]
        E_INT_TOK = ["E_int%d" % s for s in range(5)]
        EXP_TOK = ["wg0", "wu0", "wd0", "wg1", "wu1", "wd1"]
        RING_TOK = (["kT%d" % i for i in range(RK)] + ["V%d" % i for i in range(RK)] + ["qT%d" % i for i in range(RQ)]
                    + ["mixA%d" % i for i in range(RQ)] + ["mixS%d" % i for i in range(RQ)] + ["Vones"])
        SGACT_TOK = ["sg%d_%d" % (i, f) for i in range(2) for f in range(2)] + ["act%d_%d" % (i, f) for i in range(2) for f in range(2)]

        A("pool", lambda e: e.memset(nh[:], -0.5), writes=["nh"])
        ident_f = xst[0][:, 0:128]
        A("sp", lambda e: e.dma_start(out=ident_f, in_=ident_d), writes=["xst0"], kind="d", dsem="c0")
        A("dve", lambda e: e.tensor_copy(out=ident[:], in_=ident_f), reads=["xst0"], writes=["ident"])
        A("sp", lambda e: e.dma_start(out=ident_f, in_=jmat_d), reads=["ident"], writes=["xst0"], kind="d", dsem="c13")
        A("dve", lambda e: e.tensor_copy(out=jmat[:], in_=ident_f), reads=["xst0"], writes=["jmat"])
        A("sp", lambda e: e.dma_start(out=sel[:], in_=sel_d), writes=["sel"], kind="d", dsem="c14")

        def ln_tile(m, g_ap, b_ap, gtoks, eps=EPS):
            xr = x_res[:, m, :]
            t = "xres%d" % m
            o = 128 * (m % 2)
            sfx = "_%d" % (m % 2)
            A("dve", lambda e: e.bn_stats(out=sm[:, o:o + 6], in_=xr[:, 0:512]), reads=[t], writes=["st0" + sfx])
            A("dve", lambda e: e.bn_stats(out=sm[:, o + 6:o + 12], in_=xr[:, 512:1024]), reads=[t], writes=["st1" + sfx])
            A("dve", lambda e: e.bn_aggr(out=sm[:, o + 12:o + 14], in_=sm[:, o:o + 12]), reads=["st0" + sfx, "st1" + sfx], writes=["mv" + sfx])
            A("dve", lambda e: e.tensor_scalar(out=sm[:, o + 14:o + 15], in0=sm[:, o + 13:o + 14], scalar1=eps, scalar2=None, op0=ALU.add),
              reads=["mv" + sfx], writes=["ve" + sfx])
            A("pool", lambda e: e.tensor_tensor(out=sm[:, o + 15:o + 16], in0=sm[:, o + 14:o + 15], in1=nh[:, 0:1], op=ALU.pow),
              reads=["ve" + sfx, "nh"], writes=["rstd" + sfx])
            A("dve", lambda e: e.scalar_tensor_tensor(out=xr, in0=xr, scalar=sm[:, o + 12:o + 13], in1=g_ap, op0=ALU.subtract, op1=ALU.mult),
              reads=[t, "mv" + sfx, gtoks[0]], writes=[t])
            A("dve", lambda e: e.scalar_tensor_tensor(out=xr, in0=xr, scalar=sm[:, o + 15:o + 16], in1=b_ap, op0=ALU.mult, op1=ALU.add),
              reads=[t, "rstd" + sfx, gtoks[1]], writes=[t])

        def rms_to(src_ap, src_tok, dst_ap, dst_tok, col, jk):
            A("act", lambda e: e.activation(out=jk, in_=src_ap, func=AF.Square, accum_out=sm[:, col:col + 1]),
              reads=list(src_tok), writes=["ss%d" % col])
            A("dve", lambda e: e.tensor_scalar(out=sm[:, col + 1:col + 2], in0=sm[:, col:col + 1], scalar1=1.0 / 512, scalar2=EPS,
                                               op0=ALU.mult, op1=ALU.add), reads=["ss%d" % col], writes=["sv%d" % col])
            A("pool", lambda e: e.tensor_tensor(out=sm[:, col + 2:col + 3], in0=sm[:, col + 1:col + 2], in1=nh[:, 0:1], op=ALU.pow),
              reads=["sv%d" % col, "nh"], writes=["sr%d" % col])
            A("dve", lambda e: e.tensor_scalar(out=dst_ap, in0=src_ap, scalar1=sm[:, col + 2:col + 3], scalar2=None, op0=ALU.mult),
              reads=list(src_tok) + ["sr%d" % col], writes=[dst_tok])

        def transpose_to(src_ap, src_tok, evac, bank):
            for k in range(8):
                A("pe", lambda e, k=k: e.transpose(out=pTv[bank][:, k, :], in_=src_ap[:, k * 128:(k + 1) * 128], identity=ident[:]),
                  reads=list(src_tok) + ["ident"], writes=["B%d" % bank], glue=True)
            evac(pTv[bank], "B%d" % bank)

        def layer_consts(l):
            A("pool", lambda e: e.dma_start(out=wsT[:], in_=wsT_d[l].rearrange("p (g q) -> p g q", g=8)), writes=["wsT"], kind="d", dsem="c1")
            A("pool", lambda e: e.dma_start(out=wr[:], in_=wr_d[l].rearrange("(k p) n -> p k n", p=128)), writes=["wr"], kind="d", dsem="c2")
            A("sp", lambda e: e.dma_start(out=bs[:], in_=bs_d[l]), writes=["bs"], kind="d", dsem="c4")
            A("sp", lambda e: e.dma_start(out=mixg[:], in_=mixg_d[l]), writes=["mixg"], kind="d", dsem="c5")
            A("sp", lambda e: e.dma_start(out=sgug[:], in_=sgug_d[l].partition_broadcast(128)), writes=["sgug"], kind="d", dsem="c6")
            A("sp", lambda e: e.dma_start(out=sgub[:], in_=sgub_d[l].partition_broadcast(128)), writes=["sgub"], kind="d", dsem="c7")
            if l == 0:
                A("sp", lambda e: e.dma_start(out=ln1g[:], in_=ln1g_d[l].partition_broadcast(128)), writes=["ln1g"], kind="d", dsem="c8")
                A("sp", lambda e: e.dma_start(out=ln1b[:], in_=ln1b_d[l].partition_broadcast(128)), writes=["ln1b"], kind="d", dsem="c9")
            A("sp", lambda e: e.dma_start(out=br[:], in_=br_d[l].partition_broadcast(128)), writes=["br"], kind="d", dsem="c10")
            for s in range(5):
                A("sp", lambda e, s=s: e.dma_start(out=xst[s % 2][:], in_=bm_int_d[l, :, s * 1024:(s + 1) * 1024]),
                  writes=["xst%d" % (s % 2)], kind="d", dsem="ei%d" % (s % 2))
                A("act", lambda e, s=s: e.activation(out=E_int[:, s, :, :].rearrange("p h q -> p (h q)"), in_=xst[s % 2][:], func=AF.Exp),
                  reads=["xst%d" % (s % 2)], writes=["E_int%d" % s])

        def mixer_weight_chunk(l, q4):
            if q4 < 4:
                A("pool", lambda e: e.dma_start(out=w_in[:, 2 * q4:2 * q4 + 2, :],
                                                in_=w_in_d[l, q4 * 256:(q4 + 1) * 256, :].rearrange("(k p) n -> p k n", p=128)),
                  writes=["w_in_l%d" % q4], kind="d", dsem="win%d" % q4)
            else:
                A("pool", lambda e: e.dma_start(out=w_out[:], in_=w_out_d[l].rearrange("(k p) n -> p k n", p=128)),
                  writes=["w_out"], kind="d", dsem="c3")

        def arena_barrier():
            A("pool", lambda e: e.memset(sm[:, 127:128], 0.0), writes=RING_TOK + EXP_TOK + SGACT_TOK)

        def subblock(l, sbk, src_tile_ap, dst_tile_ap, dst_tok, last_layer, prefetch, pending_ln2):
            if sbk == 0:
                kv_tiles = list(range(0, 10))
                q_off = 0
            else:
                kv_tiles = list(range(6, 18))
                q_off = 2
            n_kv = len(kv_tiles)

            def win(m):
                if sbk == 0:
                    return [0, 1, 2, 3] if m < 2 else list(range(m - 2, m + 3))
                return list(range(m, m + 5))

            in_prologue = [True]
            S.capture_begin()
            arena_barrier()
            A("pool", lambda e: e.memset(Vt[:, :, :, 64:65], 1.0), writes=["Vones"])

            def A_kv(j):
                tile = kv_tiles[j]
                m = j - q_off
                own = 0 <= m < n_q
                r2 = j % 2
                r3 = j % 3
                rk = j % RK
                rev = (l > 0 and tile >= N_OWN)
                if own:
                    xsrc, xtok = x_res[:, m, :], "xres%d" % m
                else:
                    xsrc, xtok = xst[r2][:], "xst%d" % r2
                if not rev:
                    sap, stok = src_tile_ap(tile)
                    rq_ = (pending_ln2[1][m] if (own and in_prologue[0] and pending_ln2 is not None) else None)
                    A("sp", lambda e: e.dma_start(out=xsrc, in_=sap), reads=[stok], writes=[xtok], kind="d", dsem="xl%d" % j, req=rq_)
                    A("act", lambda e: e.copy(out=xb[r2][:], in_=xsrc), reads=[xtok], writes=["xb%d" % r2])
                    transpose_to(xb[r2][:], ["xb%d" % r2],
                                 lambda pv, tk: A("act", lambda e: e.copy(out=xT[r3][:], in_=pv), reads=[tk], writes=["xT%d" % r3]), 1)
                else:
                    o = (17 - tile) * 128
                    cco = cc_out.ap()
                    A("sp", lambda e: e.dma_start(out=xst[0][:], in_=cco[o:o + 128, :]), reads=["gath"], writes=["xst0"], kind="d", dsem="ga")
                    A("sp", lambda e: e.dma_start(out=xst[1][:], in_=cco[256 + o:256 + o + 128, :]), reads=["gath"], writes=["xst1"], kind="d", dsem="gb")
                    A("dve", lambda e: e.tensor_scalar(out=xst[0][:], in0=xst[0][:], scalar1=sel[:, 0:1], scalar2=None, op0=ALU.mult),
                      reads=["xst0", "sel"], writes=["xst0"])
                    A("dve", lambda e: e.scalar_tensor_tensor(out=xst[0][:], in0=xst[1][:], scalar=sel[:, 1:2], in1=xst[0][:],
                                                              op0=ALU.mult, op1=ALU.add), reads=["xst0", "xst1", "sel"], writes=["xst0"])
                    A("act", lambda e: e.copy(out=xb[r2][:], in_=xst[0][:]), reads=["xst0"], writes=["xb%d" % r2])
                    for hf in range(2):
                        for k4 in range(4):
                            k = hf * 4 + k4
                            A("pe", lambda e, k=k, k4=k4: e.matmul(B[1][:, k4 * 128:(k4 + 1) * 128], lhsT=xb[r2][:, k * 128:(k + 1) * 128],
                                                                   rhs=jmat[:], start=True, stop=True), reads=["xb%d" % r2, "jmat"], writes=["B1"])
                        A("act", lambda e, hf=hf: e.copy(out=xT[r3][:, hf * 4:(hf + 1) * 4, :], in_=B[1][:].rearrange("p (k t) -> p k t", k=4)),
                          reads=["B1"], writes=["xT%d_%d" % (r3, hf)])
                xt = xT[r3]
                xtt = ["xT%d" % r3, "xT%d_0" % r3, "xT%d_1" % r3]
                for k in range(8):
                    A("pe", lambda e, k=k: e.matmul(B[1][:], lhsT=xt[:, k, :], rhs=w_in[:, k, 512:1024], start=(k == 0), stop=(k == 7)),
                      reads=xtt + W_IN_TOK, writes=["B1"])
                A("act", lambda e: e.copy(out=kb_[:], in_=B[1][:]), reads=["B1"], writes=["kb_"])
                for c in range(4):
                    A("pe", lambda e, c=c: e.transpose(out=pTv[1][:, c, :], in_=kb_[:, c * 128:(c + 1) * 128], identity=ident[:]),
                      reads=["kb_", "ident"], writes=["B1"])
                A("act", lambda e: e.copy(out=kT[:, rk, :, :], in_=pTv[1][:, 0:4, :]), reads=["B1"], writes=["kT%d" % rk])
                for k in range(8):
                    A("pe", lambda e, k=k: e.matmul(B[1][:], lhsT=xt[:, k, :], rhs=w_in[:, k, 1024:1536], start=(k == 0), stop=(k == 7)),
                      reads=xtt + W_IN_TOK, writes=["B1"])
                A("act", lambda e: e.copy(out=Vt[:, rk, :, 0:64], in_=B[1][:].rearrange("p (h d) -> p h d", h=8)),
                  reads=["B1", "Vones"], writes=["V%d" % rk])

            def A_q(m):
                j = m + q_off
                r3 = j % 3
                xt = xT[r3]
                xtt = ["xT%d" % r3, "xT%d_0" % r3, "xT%d_1" % r3]
                rq = m % RQ
                gu, gv = gu_[m % 2], gv_[m % 2]
                for k in range(8):
                    A("pe", lambda e, k=k: e.matmul(B[2][:], lhsT=xt[:, k, :], rhs=w_in[:, k, 0:512], start=(k == 0), stop=(k == 7)),
                      reads=xtt + W_IN_TOK, writes=["B2"])
                A("act", lambda e: e.activation(out=qb_[:], in_=B[2][:], func=AF.Copy, scale=0.125), reads=["B2"], writes=["qb_"])
                for c in range(4):
                    A("pe", lambda e, c=c: e.transpose(out=pTv[2][:, c, :], in_=qb_[:, c * 128:(c + 1) * 128], identity=ident[:]),
                      reads=["qb_", "ident"], writes=["B2"])
                A("act", lambda e: e.copy(out=qT[:, rq, :, :], in_=pTv[2][:, 0:4, :]), reads=["B2"], writes=["qT%d" % rq])
                for k in range(8):
                    A("pe", lambda e, k=k: e.matmul(B[2][:], lhsT=xt[:, k, :], rhs=w_in[:, k, 1536:2048], start=(k == 0), stop=(k == 7)),
                      reads=xtt + W_IN_TOK, writes=["B2"])
                A("act", lambda e: e.activation(out=gu[:], in_=B[2][:], func=AF.Gelu_apprx_tanh), reads=["B2"], writes=["gu%d" % (m % 2)])
                for k in range(8):
                    A("pe", lambda e, k=k: e.matmul(B[2][:], lhsT=xt[:, k, :], rhs=w_in[:, k, 2048:2560], start=(k == 0), stop=(k == 7)),
                      reads=xtt + W_IN_TOK, writes=["B2"])
                A("act", lambda e: e.activation(out=gv[:], in_=B[2][:], func=AF.Gelu_apprx_tanh), reads=["B2"], writes=["gv%d" % (m % 2)])

            def sgu_chain(m):
                rq = m % RQ
                gu, gv = gu_[m % 2], gv_[m % 2]
                tgv, tgu = "gv%d" % (m % 2), "gu%d" % (m % 2)
                gv3 = gv[:].rearrange("p (g d) -> p g d", g=8)
                sq3 = sq[:].rearrange("p (g d) -> p g d", g=8)
                A("dve", lambda e: e.tensor_reduce(out=sm[:, 32:40], in_=gv3, axis=AX.X, op=ALU.add), reads=[tgv], writes=["s1"])
                A("dve", lambda e: e.tensor_tensor(out=sq[:], in0=gv[:], in1=gv[:], op=ALU.mult), reads=[tgv], writes=["sq"])
                A("dve", lambda e: e.tensor_reduce(out=sm[:, 40:48], in_=sq3, axis=AX.X, op=ALU.add), reads=["sq"], writes=["s2"])
                A("dve", lambda e: e.tensor_scalar(out=sm[:, 48:56], in0=sm[:, 32:40], scalar1=1.0 / 64, scalar2=None, op0=ALU.mult),
                  reads=["s1"], writes=["gmean"])
                A("dve", lambda e: e.tensor_tensor(out=sm[:, 56:64], in0=sm[:, 48:56], in1=sm[:, 48:56], op=ALU.mult),
                  reads=["gmean"], writes=["gmsq"])
                A("dve", lambda e: e.scalar_tensor_tensor(out=sm[:, 64:72], in0=sm[:, 40:48], scalar=1.0 / 64, in1=sm[:, 56:64],
                                                          op0=ALU.mult, op1=ALU.subtract), reads=["s2", "gmsq"], writes=["gvar"])
                A("dve", lambda e: e.tensor_scalar(out=sm[:, 72:80], in0=sm[:, 64:72], scalar1=EPS, scalar2=None, op0=ALU.add),
                  reads=["gvar"], writes=["gve"])
                A("pool", lambda e: e.tensor_tensor(out=sm[:, 80:88], in0=sm[:, 72:80], in1=nh[:], op=ALU.pow),
                  reads=["gve", "nh"], writes=["grstd"])
                A("dve", lambda e: e.tensor_tensor(out=gv3, in0=gv3, in1=sm[:, 48:56].unsqueeze(2).to_broadcast([128, 8, 64]), op=ALU.subtract),
                  reads=[tgv, "gmean"], writes=[tgv])
                A("dve", lambda e: e.tensor_tensor(out=gv3, in0=gv3, in1=sm[:, 80:88].unsqueeze(2).to_broadcast([128, 8, 64]), op=ALU.mult),
                  reads=[tgv, "grstd"], writes=[tgv])
                A("dve", lambda e: e.tensor_tensor(out=gv[:], in0=gv[:], in1=sgug[:], op=ALU.mult), reads=[tgv, "sgug"], writes=[tgv])
                A("dve", lambda e: e.tensor_tensor(out=vn[:], in0=gv[:], in1=sgub[:], op=ALU.add), reads=[tgv, "sgub"], writes=["vn"])
                for g in range(8):
                    A("pe", lambda e, g=g: e.matmul(B[0][:, g * 64:(g + 1) * 64], lhsT=wsT[:, g, :], rhs=vn[:, g * 64:(g + 1) * 64],
                                                    start=True, stop=True), reads=["vn", "wsT"], writes=["B0"])
                A("dve", lambda e: e.tensor_tensor(out=gv3, in0=B[0][:].rearrange("p (g d) -> p g d", g=8),
                                                   in1=bs[:].unsqueeze(2).to_broadcast([128, 8, 64]), op=ALU.add),
                  reads=["B0", "bs", "vn"], writes=[tgv])
                A("dve", lambda e: e.tensor_tensor(out=gv[:], in0=gv[:], in1=gu[:], op=ALU.mult), reads=[tgv, tgu], writes=[tgv])
                rms_to(gv[:], [tgv], mixed[:, rq, 512:1024], "mixS%d" % rq, 88, junk[:])

            def step_B(m):
                rq = m % RQ
                W = win(m)
                nW = len(W)
                edge = (sbk == 0 and m < 2)
                po = [B[6][:, 0:260].rearrange("p (h d) -> p h d", h=4), B[7][:, 0:260].rearrange("p (h d) -> p h d", h=4)]

                def scores(h, part):
                    c, pb = h // 2, (h % 2) * 64
                    for i, kvj in enumerate(W):
                        if (i < 4) != (part == 0):
                            continue
                        if i < 4:
                            oap, tok = B[4][:, i * 128:(i + 1) * 128], "B4"
                        else:
                            oap, tok = B[5][:, 0:128], "B5"
                        A("pe", lambda e, oap=oap, kvj=kvj, pb=pb, c=c: e.matmul(
                            oap, lhsT=kT[pb:pb + 64, kvj % RK, c, :], rhs=qT[pb:pb + 64, rq, c, :], start=True, stop=True),
                          reads=["kT%d" % (kvj % RK), "qT%d" % rq], writes=[tok])

                def softmax_pv(h, mid):
                    hp = h % 2
                    mb = 4
                    ei = 0
                    if edge:
                        idx = m * 8 + h
                        ei = idx % 2
                        A("sp", lambda e, idx=idx, ei=ei: e.dma_start(out=est[ei][:], in_=bm_edge_d[l, idx]), writes=["est0"],
                          kind="d", dsem="ee0")
                        A("act", lambda e, ei=ei: e.activation(out=eed[ei][:], in_=est[ei][:], func=AF.Exp), reads=["est0"],
                          writes=["eed0"])
                    A("act", lambda e, hp=hp, mb=mb: e.activation(out=pexp[hp][:, 0:512], in_=B[mb][:], func=AF.Exp), reads=["B%d" % mb],
                      writes=["pexpa%d" % hp])
                    ptoks = ["pexpa%d" % hp]
                    if nW > 4:
                        A("act", lambda e, hp=hp: e.activation(out=pexp[hp][:, 512:640], in_=B[5][:, 0:128], func=AF.Exp),
                          reads=["B5"], writes=["pexpb%d" % hp])
                        ptoks.append("pexpb%d" % hp)
                    mid()
                    if edge:
                        A("pool", lambda e, hp=hp, ei=ei: e.tensor_tensor(out=pTs[hp][:, 0:512], in0=pexp[hp][:, 0:512],
                                                                           in1=eed[ei][:, 0:512], op=ALU.mult),
                          reads=ptoks + ["eed0"], writes=["pTs%d" % hp])
                    else:
                        A("dve", lambda e, hp=hp, h=h: e.tensor_tensor(out=pTs[hp][:, 0:640].rearrange("p (s q) -> p s q", s=5),
                                                                         in0=pexp[hp][:, 0:640].rearrange("p (s q) -> p s q", s=5),
                                                                         in1=E_int[:, :, h, :], op=ALU.mult),
                          reads=ptoks + E_INT_TOK, writes=["pTs%d" % hp])
                    pbk = 6 + h // 4
                    for i, kvj in enumerate(W):
                        A("pe", lambda e, i=i, kvj=kvj, hp=hp, h=h: e.matmul(po[h // 4][:, h % 4, :], lhsT=pTs[hp][:, i * 128:(i + 1) * 128],
                                                                               rhs=Vt[:, kvj % RK, h, :], start=(i == 0), stop=(i == nW - 1)),
                          reads=["pTs%d" % hp, "V%d" % (kvj % RK)], writes=["B%d" % pbk])

                scores(0, 0)
                scores(0, 1)
                for h in range(8):
                    if h + 1 < 8:
                        softmax_pv(h, lambda h=h: (scores(h + 1, 0), scores(h + 1, 1)))
                    else:
                        softmax_pv(h, lambda: None)
                for hh in range(2):
                    A("dve", lambda e, hh=hh: e.reciprocal(out=sm[:, 96 + hh * 4:100 + hh * 4], in_=po[hh][:, :, 64]), reads=["B%d" % (6 + hh)],
                      writes=["rec%d" % hh])
                    A("dve", lambda e, hh=hh: e.tensor_tensor(out=a_o[:, hh * 256:(hh + 1) * 256].rearrange("p (h d) -> p h d", h=4),
                                                              in0=po[hh][:, :, 0:64],
                                                              in1=sm[:, 96 + hh * 4:100 + hh * 4].unsqueeze(2).to_broadcast([128, 4, 64]), op=ALU.mult),
                      reads=["B%d" % (6 + hh), "rec%d" % hh], writes=["a_o%d" % hh])
                rms_to(a_o[:], ["a_o0", "a_o1"], mixed[:, rq, 0:512], "mixA%d" % rq, 104, junk[:])

            def step_C(m):
                rq = m % RQ
                t = "xres%d" % m

                def evac_m(pv, tk):
                    A("dve", lambda e: e.tensor_tensor(out=mT[:], in0=pv, in1=mixg[:].unsqueeze(2).to_broadcast([128, 8, 128]), op=ALU.mult),
                      reads=[tk, "mixg"], writes=["mT"])
                transpose_to(mixed[:, rq, :], ["mixA%d" % rq, "mixS%d" % rq], evac_m, 3)
                for half in range(2):
                    for k in range(8):
                        A("pe", lambda e, half=half, k=k: e.matmul(B[3][:], lhsT=mT[:, k, :], rhs=w_out[:, k, half * 512:(half + 1) * 512],
                                                                   start=(k == 0), stop=(k == 7)), reads=["mT", "w_out"], writes=["B3"])
                    A("dve", lambda e, half=half: e.scalar_tensor_tensor(out=x_res[:, m, half * 512:(half + 1) * 512],
                                                                         in0=x_res[:, m, half * 512:(half + 1) * 512], scalar=ALPHA,
                                                                         in1=B[3][:], op0=ALU.mult, op1=ALU.add),
                      reads=[t, "B3"], writes=[t])
                ln_tile(m, ln1g[:], ln1b[:], ["ln1g", "ln1b"])
                r2 = m % 2
                A("act", lambda e: e.copy(out=xbc[:], in_=x_res[:, m, :]), reads=[t], writes=["xbc"])
                transpose_to(xbc[:], ["xbc"],
                             lambda pv, tk: A("act", lambda e: e.copy(out=x1T[:, :, m * 128:(m + 1) * 128], in_=pv), reads=[tk], writes=["x1T%d" % m]), 3)
                for k in range(8):
                    A("pe", lambda e, k=k: e.matmul(B[3][:, 0:20], lhsT=x1T[:, k, m * 128:(m + 1) * 128], rhs=wr[:, k, :],
                                                    start=(k == 0), stop=(k == 7)), reads=["x1T%d" % m, "wr"], writes=["B3"])
                rs = rscr[m % 2]
                A("dve", lambda e: e.tensor_tensor(out=rs[:, 0:20], in0=B[3][:, 0:20], in1=br[:], op=ALU.add), reads=["B3", "br"],
                  writes=["rscr%d" % (m % 2)])

            def router_chain(m):
                rs = rscr[m % 2]
                tk = ["rscr%d" % (m % 2)]
                R = lambda a, b: rs[:, a:b]
                D = lambda fn: A("dve", fn, reads=tk, writes=tk)
                D(lambda e: e.tensor_reduce(out=R(20, 21), in_=R(0, 4), axis=AX.X, op=ALU.max))
                D(lambda e: e.tensor_scalar(out=R(21, 22), in0=R(20, 21), scalar1=-1.0, scalar2=None, op0=ALU.mult))
                A("act", lambda e: e.activation(out=R(24, 28), in_=R(0, 4), func=AF.Exp, bias=R(21, 22), scale=1.0, accum_out=R(22, 23)),
                  reads=tk, writes=tk)
                D(lambda e: e.reciprocal(out=R(23, 24), in_=R(22, 23)))
                D(lambda e: e.tensor_scalar(out=R(28, 32), in0=R(0, 4), scalar1=R(20, 21), scalar2=None, op0=ALU.is_equal))
                D(lambda e: e.tensor_tensor(out=rs[:, 32:48].rearrange("p (g e) -> p g e", g=4),
                                            in0=rs[:, 4:20].rearrange("p (g e) -> p g e", g=4),
                                            in1=R(28, 32).unsqueeze(2).to_broadcast([128, 4, 4]), op=ALU.mult))
                D(lambda e: e.tensor_reduce(out=R(48, 52), in_=rs[:, 32:48].rearrange("p (g e) -> p e g", g=4), axis=AX.X, op=ALU.add))
                D(lambda e: e.tensor_reduce(out=R(52, 53), in_=R(48, 52), axis=AX.X, op=ALU.max))
                D(lambda e: e.tensor_scalar(out=R(56, 60), in0=R(48, 52), scalar1=R(52, 53), scalar2=None, op0=ALU.is_equal))
                D(lambda e: e.scalar_tensor_tensor(out=R(60, 64), in0=R(56, 60), scalar=-1e30, in1=R(48, 52), op0=ALU.mult, op1=ALU.add))
                D(lambda e: e.tensor_reduce(out=R(53, 54), in_=R(60, 64), axis=AX.X, op=ALU.max))
                D(lambda e: e.tensor_scalar(out=R(64, 68), in0=R(60, 64), scalar1=R(53, 54), scalar2=None, op0=ALU.is_equal))
                D(lambda e: e.tensor_tensor(out=R(54, 55), in0=R(53, 54), in1=R(52, 53), op=ALU.subtract))
                A("act", lambda e: e.activation(out=R(55, 56), in_=R(54, 55), func=AF.Exp), reads=tk, writes=tk)
                D(lambda e: e.tensor_scalar(out=R(68, 69), in0=R(55, 56), scalar1=1.0, scalar2=None, op0=ALU.add))
                D(lambda e: e.reciprocal(out=R(69, 70), in_=R(68, 69)))
                D(lambda e: e.tensor_tensor(out=R(70, 71), in0=R(69, 70), in1=R(55, 56), op=ALU.mult))
                D(lambda e: e.tensor_scalar(out=R(72, 76), in0=R(56, 60), scalar1=R(69, 70), scalar2=None, op0=ALU.mult))
                D(lambda e: e.scalar_tensor_tensor(out=R(76, 80), in0=R(64, 68), scalar=R(70, 71), in1=R(72, 76), op0=ALU.mult, op1=ALU.add))
                D(lambda e: e.tensor_scalar(out=R(80, 84), in0=R(76, 80), scalar1=R(23, 24), scalar2=1.0 / ALPHA, op0=ALU.mult, op1=ALU.mult))
                A("dve", lambda e: e.tensor_tensor(out=comb[:, m, :].rearrange("p (g e) -> p g e", g=4),
                                                   in0=R(28, 32).unsqueeze(2).to_broadcast([128, 4, 4]),
                                                   in1=R(80, 84).unsqueeze(1).to_broadcast([128, 4, 4]), op=ALU.mult), reads=tk, writes=["comb%d" % m])

            aseq = []
            for j in range(n_kv):
                mq = j - q_off - 2
                if 0 <= mq < n_q:
                    aseq.append(("q", mq))
                aseq.append(("kv", j))
            for mq in range(n_q):
                if ("q", mq) not in aseq:
                    aseq.append(("q", mq))
            apos = [0]

            def emit_A_until(kv_need, q_need, lists=None):
                def have():
                    donev = [u for u in aseq[:apos[0]]]
                    return (("kv", kv_need) in donev or kv_need < 0) and (("q", q_need) in donev or q_need < 0)
                while not have():
                    kind, idx = aseq[apos[0]]
                    apos[0] += 1
                    if lists is None:
                        (A_kv if kind == "kv" else A_q)(idx)
                    else:
                        S.capture_begin()
                        (A_kv if kind == "kv" else A_q)(idx)
                        lists[kind].extend(S.capture_end())

            def cap(fn):
                S.capture_begin()
                fn()
                return S.capture_end()

            emit_A_until(max(win(0)), 0)
            prologue = S.capture_end()
            in_prologue[0] = False
            S.commit_interleaved([pending_ln2[0] if pending_ln2 is not None else [], prologue])
            for m in range(n_q):
                lc = cap(lambda: (step_C(m - 1) if m >= 1 else None))
                la = {"kv": [], "q": []}
                if m + 1 < n_q:
                    emit_A_until(max(win(m + 1)), m + 1, la)
                else:
                    emit_A_until(n_kv - 1, n_q - 1, la)
                ly = cap(lambda: step_B(m))
                lz = cap(lambda: (router_chain(m - 2) if m >= 2 else None))
                lg = cap(lambda: sgu_chain(m))
                S.commit_interleaved([ly, lc, la["q"], la["kv"], lz, lg])
            def expert_dma(ex):
                ebi = ex % 2
                A("pool", lambda e: e.dma_start(out=wg_b[ebi], in_=wg_d[l, ex].rearrange("(k p) f -> p k f", p=128)),
                  writes=["wg%d" % ebi], kind="d", dsem="ewg%d" % ebi)
                A("pool", lambda e: e.dma_start(out=wu_b[ebi], in_=wu_d[l, ex].rearrange("(k p) f -> p k f", p=128)),
                  writes=["wu%d" % ebi], kind="d", dsem="ewu%d" % ebi)
                A("pool", lambda e: e.dma_start(out=wd_b[ebi], in_=wd_d[l, ex].rearrange("(c p) n -> p c n", p=128)),
                  writes=["wd%d" % ebi], kind="d", dsem="ewd%d" % ebi)

            def early_experts():
                A("pool", lambda e: e.memset(sm[:, 127:128], 0.0),
                  writes=[t for t in RING_TOK if t not in ("mixA2", "mixA3", "mixS2", "mixS3")] + EXP_TOK)
                expert_dma(0)
                expert_dma(1)

            lx = cap(lambda: step_C(n_q - 1))
            lz = cap(lambda: router_chain(n_q - 2))
            le = cap(early_experts)
            S.commit_interleaved([lx, lz, le])
            router_chain(n_q - 1)
            assert apos[0] == len(aseq)

            n_tg = n_q // 4
            yrot = [5, 6, 7, 0]
            yi = [0]

            def emit_down(ex, tg, ri, ebi):
                for t in range(4):
                    tile = tg * 4 + t
                    tt = "xres%d" % tile
                    for half in range(2):
                        yb = yrot[yi[0] % 4]
                        yi[0] += 1
                        for fc in range(2):
                            A("pe", lambda e, fc=fc, t=t, half=half, yb=yb, ri=ri, ebi=ebi: e.matmul(
                                B[yb][:], lhsT=act_b[ri][:, fc, t * 128:(t + 1) * 128], rhs=wd_b[ebi][:, fc, half * 512:(half + 1) * 512],
                                start=(fc == 0), stop=(fc == 1)), reads=["act%d_0" % ri, "act%d_1" % ri, "wd%d" % ebi],
                              writes=["B%d" % yb])
                        A("dve", lambda e, tile=tile, half=half, yb=yb, ex=ex: e.scalar_tensor_tensor(
                            out=x_res[:, tile, half * 512:(half + 1) * 512], in0=B[yb][:], scalar=comb[:, tile, ex:ex + 1],
                            in1=x_res[:, tile, half * 512:(half + 1) * 512], op0=ALU.mult, op1=ALU.add),
                          reads=["B%d" % yb, tt, "comb%d" % tile], writes=[tt])

            A("pool", lambda e: e.memset(sm[:, 126:127], 0.0), writes=["mixA2", "mixA3", "mixS2", "mixS3"] + SGACT_TOK)
            pending = None
            hcount = 0
            for ex in range(16):
                ebi = ex % 2
                if prefetch is not None and ex in (2, 5, 8, 11, 13):
                    mixer_weight_chunk(prefetch, (2, 5, 8, 11, 13).index(ex))
                if ex >= 2:
                    expert_dma(ex)
                for tg in range(n_tg):
                    ri = (ex * n_tg + tg) % 2
                    x1toks = ["x1T%d" % (tg * 4 + t) for t in range(4)]
                    for fc in range(2):
                        hs = hcount % 2
                        hcount += 1
                        gb, ub = (1, 2) if hs == 0 else (3, 4)
                        for k in range(8):
                            A("pe", lambda e, fc=fc, k=k, ebi=ebi, tg=tg, gb=gb: e.matmul(B[gb][:], lhsT=wg_b[ebi][:, k, fc * 128:(fc + 1) * 128],
                                                                                          rhs=x1T[:, k, tg * 512:(tg + 1) * 512], start=(k == 0), stop=(k == 7)),
                              reads=x1toks + ["wg%d" % ebi], writes=["B%d" % gb])
                        for k in range(8):
                            A("pe", lambda e, fc=fc, k=k, ebi=ebi, tg=tg, ub=ub: e.matmul(B[ub][:], lhsT=wu_b[ebi][:, k, fc * 128:(fc + 1) * 128],
                                                                                          rhs=x1T[:, k, tg * 512:(tg + 1) * 512], start=(k == 0), stop=(k == 7)),
                              reads=x1toks + ["wu%d" % ebi], writes=["B%d" % ub])
                        A("act", lambda e, hs=hs, gb=gb: e.activation(out=sg_b[hs][:, 0, :], in_=B[gb][:], func=AF.Silu),
                          reads=["B%d" % gb], writes=["sg%d_0" % hs])
                        A("dve", lambda e, fc=fc, ri=ri, hs=hs, ub=ub: e.tensor_tensor(out=act_b[ri][:, fc, :], in0=sg_b[hs][:, 0, :], in1=B[ub][:], op=ALU.mult),
                          reads=["sg%d_0" % hs, "B%d" % ub], writes=["act%d_%d" % (ri, fc)])
                        if fc == 0 and pending is not None:
                            emit_down(*pending)
                            pending = None
                    pending = (ex, tg, ri, ebi)
            emit_down(*pending)

            S.capture_begin()
            A("sp", lambda e: e.dma_start(out=ln1g[:], in_=ln2g_d[l].partition_broadcast(128)), writes=["ln1g"], kind="d", dsem="c11")
            A("sp", lambda e: e.dma_start(out=ln1b[:], in_=ln2b_d[l].partition_broadcast(128)), writes=["ln1b"], kind="d", dsem="c12")
            ends = {}
            for m in range(n_q):
                ln_tile(m, ln1g[:], ln1b[:], ["ln1g", "ln1b"], eps=EPS / (ALPHA * ALPHA))
                gt = sbk * 8 + m
                dap = dst_tile_ap(gt)
                A("sp", lambda e, m=m, dap=dap: e.dma_start(out=dap, in_=x_res[:, m, :]), reads=["xres%d" % m],
                  writes=[dst_tok(gt)], kind="d", dsem="xw%d" % gt)
                if (not last_layer) and gt >= 14 and use_cc:
                    ci = cc_in.ap()
                    A("sp", lambda e, m=m, gt=gt, ci=ci: e.dma_start(out=ci[(gt - 14) * 128:(gt - 13) * 128, :], in_=x_res[:, m, :]),
                      reads=["xres%d" % m], writes=["ccin%d" % (gt - 14)], kind="d", dsem="xw%d" % (gt + 2))
                ends[m] = len(S._cap)
            ln_next = l if sbk == 0 else (l + 1 if not last_layer else None)
            if ln_next is not None:
                A("sp", lambda e: e.dma_start(out=ln1g[:], in_=ln1g_d[ln_next].partition_broadcast(128)), writes=["ln1g"], kind="d", dsem="c8")
                A("sp", lambda e: e.dma_start(out=ln1b[:], in_=ln1b_d[ln_next].partition_broadcast(128)), writes=["ln1b"], kind="d", dsem="c9")
            if (not last_layer) and sbk == 1 and use_cc:
                A("pool", lambda e: e.collective_compute("AllGather", ALU.bypass, replica_groups=[[0, 1], [2, 3], [4, 5], [6, 7]],
                                                         ins=[cc_in.ap().opt()], outs=[cc_out.ap().opt()]),
                  reads=["ccin0", "ccin1"], writes=["gath"], kind="cc", dsem="cc%d" % l)
            return (S.capture_end(), ends)

        pend = [None]
        for l in range(depth):
            last = (l == depth - 1)
            if l == 0:
                src = lambda t: (x_d[t * 128:(t + 1) * 128, :], "x_in")
            else:
                sbuf_ = bufs[(l - 1) % 2]
                src = (lambda sb_, bi: (lambda t: (sb_[t * 128:(t + 1) * 128, :], "xd%d_%d" % (bi, t))))(sbuf_, (l - 1) % 2)
            if last:
                dst = lambda t: out_d[t * 128:(t + 1) * 128, :]
                dtok = lambda t: "outd_%d" % t
            else:
                dbuf_ = bufs[l % 2]
                dst = (lambda db_: (lambda t: db_[t * 128:(t + 1) * 128, :]))(dbuf_)
                dtok = (lambda bi: (lambda t: "xd%d_%d" % (bi, t)))(l % 2)
            layer_consts(l)
            if l == 0:
                for q4 in range(5):
                    mixer_weight_chunk(0, q4)
            for sbk in range(2):
                pend[0] = subblock(l, sbk, src, dst, dtok, last, (l + 1) if (sbk == 1 and not last) else None, pend[0])
        S.commit_interleaved([pend[0][0]])
        final = ["xw%d" % t for t in range(16)]
        S.emit(sems, dsems, final_wait=final)
    return nc


GRID_W = 64
SEQ = 4096


def _token_map(flip):
    u = np.arange((N_OWN + 2) * 128)
    return (SEQ - 1 - u) if flip else u


def _bias_tables(rel_bias, flip):
    gmap = _token_map(flip)
    H = rel_bias.shape[0]

    def slot(m, kv):
        gq = gmap[m * 128:(m + 1) * 128]
        gk = gmap[kv * 128:(kv + 1) * 128]
        rq, cq = gq // GRID_W, gq % GRID_W
        rk, ck = gk // GRID_W, gk % GRID_W
        rs = np.clip(rq - 4, 0, 56)
        cs = np.clip(cq - 8, 0, 48)
        ok = (rk[:, None] >= rs[None, :]) & (rk[:, None] < rs[None, :] + 8)
        ok &= (ck[:, None] >= cs[None, :]) & (ck[:, None] < cs[None, :] + 16)
        dr = np.clip(rk[:, None] - rq[None, :] + 7, 0, 14)
        dc = np.clip(ck[:, None] - cq[None, :] + 15, 0, 30)
        tab = rel_bias[:, dr, dc]
        return np.where(ok[None], tab, np.float32(NEG)).astype(np.float32)

    interior = np.empty((128, 5, H, 128), np.float32)
    for i in range(5):
        interior[:, i] = slot(4, 2 + i).transpose(1, 0, 2)
    edge = np.empty((2, H, 128, 4, 128), np.float32)
    for m in range(2):
        for i in range(4):
            edge[m, :, :, i, :] = slot(m, i)
    return interior.reshape(128, 5 * H * 128), edge.reshape(2 * H, 128, 512)


_PROG = {}


def _get_prog(depth):
    if depth not in _PROG:
        _PROG[depth] = build_program(depth)
    return _PROG[depth]


def kernel(x, w_in, w_out, na_rel_bias, sgu_ln_g, sgu_ln_b, sgu_w, sgu_b, mix_norm_g,
           ln1_g, ln1_b, router_group_w, router_group_b, router_expert_w, router_expert_b,
           expert_w_gate, expert_w_up, expert_w_down, ln2_g, ln2_b):
    f = lambda a: np.ascontiguousarray(np.asarray(a, dtype=np.float32))
    x = f(x)
    Bsz, T, D = x.shape
    depth = w_in.shape[0]
    nc = _get_prog(depth)
    ident = np.eye(128, dtype=np.float32)
    jm = np.ascontiguousarray(ident[::-1])
    shared = dict(
        w_in=f(w_in), w_out=f(w_out),
        sgu_g=f(sgu_ln_g).reshape(depth, 1, 512), sgu_b=f(sgu_ln_b).reshape(depth, 1, 512),
        mixg=f(np.asarray(mix_norm_g).reshape(depth, 8, 128).transpose(0, 2, 1)),
        ln1g=f(ln1_g).reshape(depth, 1, 1024), ln1b=f(ln1_b).reshape(depth, 1, 1024),
        ln2g=f(ln2_g).reshape(depth, 1, 1024), ln2b=f(ln2_b).reshape(depth, 1, 1024),
        wr=f(np.concatenate([np.asarray(router_group_w), np.asarray(router_expert_w)], axis=2)),
        br=f(np.concatenate([np.asarray(router_group_b), np.asarray(router_expert_b)], axis=1)).reshape(depth, 1, 20),
        wg=f(expert_w_gate), wu=f(expert_w_up), wd=f(expert_w_down), ident=ident, jmat=jm)
    sw = np.asarray(sgu_w, dtype=np.float32)
    sbias = np.asarray(sgu_b, dtype=np.float32)
    rb = np.asarray(na_rel_bias, dtype=np.float32)
    per_type = []
    for flip in (False, True):
        swl = sw[:, :, ::-1, ::-1] if flip else sw
        sbl = sbias[:, :, ::-1] if flip else sbias
        tabs = [_bias_tables(rb[l], flip) for l in range(depth)]
        selv = np.zeros((128, 2), np.float32)
        selv[:, 0 if flip else 1] = 1.0
        per_type.append(dict(wsT=f(swl.transpose(0, 3, 1, 2).reshape(depth, 128, 1024)), bs=f(sbl.transpose(0, 2, 1)),
                             bm_int=f(np.stack([t[0] for t in tabs])), bm_edge=f(np.stack([t[1] for t in tabs])), sel=selv))
    in_maps = []
    for c in range(8):
        b, flip = c // 2, c % 2
        gmap = _token_map(bool(flip))
        d = dict(shared)
        d.update(per_type[flip])
        d["x"] = f(x[b, gmap])
        in_maps.append(d)
    res = run_bass_kernel_spmd(nc, in_maps, core_ids=list(range(8)))
    out = np.empty_like(x)
    for c in range(8):
        b, flip = c // 2, c % 2
        gmap = _token_map(bool(flip))
        out[b, gmap[:N_OWN * 128]] = res.results[c]["out"]
    return out
```
